# Optimizing a Trainium2 kernel written in Bass

```python
import math
import jax
import jax.numpy as jnp
from jax import lax
import numpy as np

D_MODEL = 1024
BATCH = 32
SEQ = 2048
DEPTH = 1

HEAD_DIM = 64
NA_HEADS = 8
NA_WIDTH = NA_HEADS * HEAD_DIM
DIFF_HEADS = 4
DIFF_QK_WIDTH = DIFF_HEADS * 2 * HEAD_DIM
DIFF_V_DIM = 2 * HEAD_DIM
DIFF_WIDTH = DIFF_HEADS * DIFF_V_DIM
IN_COLS = 3 * NA_WIDTH + 2 * DIFF_QK_WIDTH + DIFF_WIDTH
N_BRANCH = 2
GRID_W = 64
WIN_ROWS = 8
WIN_COLS = 16
Q_BLOCK = 128
D_FF = 2816
CONV_W = 3
N_MOD = 6
EPS = 1e-6

kernel_name = "hybrid_natten_diffattn_convffn_block"


def rms_norm(x, gain):
    xf = x.astype(jnp.float32)
    y = xf * lax.rsqrt(jnp.mean(xf * xf, axis=-1, keepdims=True) + EPS)
    return (y * gain.astype(jnp.float32)).astype(x.dtype)


def alibi_slopes(n_heads):
    return jnp.asarray([2.0 ** (-8.0 * (h + 1) / n_heads) for h in range(n_heads)], dtype=jnp.float32)


def lambda_init_for(layer_idx):
    return 0.8 - 0.6 * math.exp(-0.3 * layer_idx)


def neighborhood_attention(q, k, v, rpb):
    b, s, h, dh = q.shape
    rows = s // GRID_W
    kr = min(WIN_ROWS, rows)
    kc = WIN_COLS
    qg = q.reshape(b, rows, GRID_W, h, dh)
    kg = k.reshape(b, rows, GRID_W, h, dh)
    vg = v.reshape(b, rows, GRID_W, h, dh)
    col = jnp.arange(GRID_W)
    col_idx = jnp.clip(col - kc // 2, 0, GRID_W - kc)[:, None] + jnp.arange(kc)[None, :]
    dc = col_idx - col[:, None] + (WIN_COLS - 1)
    r_idx = jnp.arange(rows)
    row_start = jnp.clip(r_idx - kr // 2, 0, rows - kr)
    scale = dh ** -0.5

    def one_row(args):
        q_row, r, r0 = args
        k_band = lax.dynamic_slice_in_dim(kg, r0, kr, axis=1)
        v_band = lax.dynamic_slice_in_dim(vg, r0, kr, axis=1)
        k_nb = k_band[:, :, col_idx]
        v_nb = v_band[:, :, col_idx]
        logits = jnp.einsum('bchd,brckhd->bhcrk', q_row, k_nb,
                            preferred_element_type=jnp.float32) * scale
        dr = r0 + jnp.arange(kr) - r + (WIN_ROWS - 1)
        bias = rpb[:, dr[None, :, None], dc[:, None, :]]
        logits = logits + bias.astype(jnp.float32)[None]
        p = jax.nn.softmax(logits.reshape(b, h, GRID_W, kr * kc), axis=-1)
        p = p.reshape(b, h, GRID_W, kr, kc).astype(v.dtype)
        return jnp.einsum('bhcrk,brckhd->bchd', p, v_nb)

    out = lax.map(one_row, (jnp.moveaxis(qg, 1, 0), r_idx, row_start))
    return jnp.moveaxis(out, 0, 1).reshape(b, s, h * dh)


def differential_attention(q, k, v, lam, lambda_init, subln_gain):
    b, s, h, _, dh = q.shape
    nb = s // Q_BLOCK
    slopes = alibi_slopes(h)
    pos = jnp.arange(s, dtype=jnp.float32)
    scale = dh ** -0.5
    qb = jnp.moveaxis(q.reshape(b, nb, Q_BLOCK, h, 2, dh), 1, 0)
    qpos = pos.reshape(nb, Q_BLOCK)

    def one_block(args):
        q_blk, tq = args
        logits = jnp.einsum('bqhcd,bkhcd->bhcqk', q_blk, k,
                            preferred_element_type=jnp.float32) * scale
        bias = -slopes[:, None, None] * jnp.abs(tq[:, None] - pos[None, :])
        p = jax.nn.softmax(logits + bias[None, :, None], axis=-1)
        a = p[:, :, 0] - lam * p[:, :, 1]
        return jnp.einsum('bhqk,bkhd->bqhd', a.astype(v.dtype), v)

    out = lax.map(one_block, (qb, qpos))
    out = jnp.moveaxis(out, 0, 1).reshape(b, s, h, -1)
    out = rms_norm(out, subln_gain) * (1.0 - lambda_init)
    return out.reshape(b, s, -1)


def depthwise_conv_centered(u, w, bias):
    pad = CONV_W // 2
    s = u.shape[1]
    up = jnp.pad(u, ((0, 0), (pad, pad), (0, 0)))
    out = up[:, 0:s] * w[0]
    for i in range(1, CONV_W):
        out = out + up[:, i:i + s] * w[i]
    return out + bias


def setup_inputs(seed: int = 0) -> dict:
    key = jax.random.key(seed)
    ks = jax.random.split(key, 32)
    D = D_MODEL
    nrm = lambda k, shape, s: jax.random.normal(k, shape, dtype=jnp.float32) * s
    gain = lambda k, shape: 1.0 + nrm(k, shape, 0.02)
    return {
        "x": nrm(ks[0], (BATCH, SEQ, D), 1.0),
        "c": nrm(ks[1], (BATCH, D), 1.0),
        "ada_w": nrm(ks[2], (DEPTH, D, N_MOD * D), 0.5 * D ** -0.5),
        "ada_b": nrm(ks[3], (DEPTH, N_MOD * D), 0.02),
        "norm1_g": gain(ks[4], (DEPTH, D)),
        "w_in": nrm(ks[5], (DEPTH, D, IN_COLS), D ** -0.5),
        "na_q_g": gain(ks[6], (DEPTH, HEAD_DIM)),
        "na_k_g": gain(ks[7], (DEPTH, HEAD_DIM)),
        "na_rpb": nrm(ks[8], (DEPTH, NA_HEADS, 2 * WIN_ROWS - 1, 2 * WIN_COLS - 1), 0.1),
        "df_q_g": gain(ks[9], (DEPTH, HEAD_DIM)),
        "df_k_g": gain(ks[10], (DEPTH, HEAD_DIM)),
        "lam_q1": nrm(ks[11], (DEPTH, HEAD_DIM), 0.1),
        "lam_k1": nrm(ks[12], (DEPTH, HEAD_DIM), 0.1),
        "lam_q2": nrm(ks[13], (DEPTH, HEAD_DIM), 0.1),
        "lam_k2": nrm(ks[14], (DEPTH, HEAD_DIM), 0.1),
        "df_subln_g": gain(ks[15], (DEPTH, DIFF_V_DIM)),
        "w_na_proj": nrm(ks[16], (DEPTH, NA_WIDTH, D), NA_WIDTH ** -0.5),
        "w_df_proj": nrm(ks[17], (DEPTH, DIFF_WIDTH, D), DIFF_WIDTH ** -0.5),
        "w_gate": nrm(ks[18], (DEPTH, D, N_BRANCH * D), D ** -0.5),
        "b_gate": nrm(ks[19], (DEPTH, N_BRANCH * D), 0.02),
        "w_out": nrm(ks[20], (DEPTH, D, D), D ** -0.5),
        "norm2_g": gain(ks[21], (DEPTH, D)),
        "w_up": nrm(ks[22], (DEPTH, D, 2 * D_FF), D ** -0.5),
        "conv_w": nrm(ks[23], (DEPTH, CONV_W, 2 * D_FF), CONV_W ** -0.5),
        "conv_b": nrm(ks[24], (DEPTH, 2 * D_FF), 0.01),
        "w_down": nrm(ks[25], (DEPTH, D_FF, D), D_FF ** -0.5),
    }


def reference(x, c, ada_w, ada_b, norm1_g, w_in, na_q_g, na_k_g, na_rpb, df_q_g, df_k_g,
              lam_q1, lam_k1, lam_q2, lam_k2, df_subln_g, w_na_proj, w_df_proj, w_gate, b_gate,
              w_out, norm2_g, w_up, conv_w, conv_b, w_down):
    b, s, _ = x.shape
    splits = [NA_WIDTH, 2 * NA_WIDTH, 3 * NA_WIDTH, 3 * NA_WIDTH + DIFF_QK_WIDTH,
              3 * NA_WIDTH + 2 * DIFF_QK_WIDTH]
    c_act = jax.nn.silu(c)
    for l in range(DEPTH):
        lam_init = lambda_init_for(l)
        mod = (jnp.einsum('bd,de->be', c_act, ada_w[l]) + ada_b[l])[:, None, :]
        sh1, sc1, g1, sh2, sc2, g2 = jnp.split(mod, N_MOD, axis=-1)

        h = rms_norm(x, norm1_g[l]) * (1.0 + sc1) + sh1
        proj = jnp.einsum('bsd,de->bse', h, w_in[l])
        na_q, na_k, na_v, df_q, df_k, df_v = jnp.split(proj, splits, axis=-1)

        na_q = rms_norm(na_q.reshape(b, s, NA_HEADS, HEAD_DIM), na_q_g[l])
        na_k = rms_norm(na_k.reshape(b, s, NA_HEADS, HEAD_DIM), na_k_g[l])
        na_v = na_v.reshape(b, s, NA_HEADS, HEAD_DIM)
        y_na = neighborhood_attention(na_q, na_k, na_v, na_rpb[l])

        df_q = rms_norm(df_q.reshape(b, s, DIFF_HEADS, 2, HEAD_DIM), df_q_g[l])
        df_k = rms_norm(df_k.reshape(b, s, DIFF_HEADS, 2, HEAD_DIM), df_k_g[l])
        df_v = df_v.reshape(b, s, DIFF_HEADS, DIFF_V_DIM)
        lam = (jnp.exp(jnp.sum(lam_q1[l].astype(jnp.float32) * lam_k1[l].astype(jnp.float32)))
               - jnp.exp(jnp.sum(lam_q2[l].astype(jnp.float32) * lam_k2[l].astype(jnp.float32)))
               + lam_init)
        y_df = differential_attention(df_q, df_k, df_v, lam, lam_init, df_subln_g[l])

        ya = jnp.einsum('bse,ed->bsd', y_na, w_na_proj[l])
        yb = jnp.einsum('bse,ed->bsd', y_df, w_df_proj[l])
        gates = jax.nn.sigmoid(jnp.einsum('bsd,de->bse', h, w_gate[l]) + b_gate[l])
        ga, gb = jnp.split(gates, N_BRANCH, axis=-1)
        mixed = jnp.einsum('bsd,de->bse', ga * ya + gb * yb, w_out[l])
        x = x + g1 * mixed

        h2 = rms_norm(x, norm2_g[l]) * (1.0 + sc2) + sh2
        u = jnp.einsum('bsd,df->bsf', h2, w_up[l])
        u = depthwise_conv_centered(u, conv_w[l], conv_b[l])
        u_act, u_lin = jnp.split(u, 2, axis=-1)
        f = jnp.einsum('bsf,fd->bsd', jax.nn.gelu(u_act, approximate=False) * u_lin, w_down[l])
        x = x + g2 * f
    return x
```

```python
import numpy as np
import concourse.bass as bass
import concourse.mybir as mybir

F32 = mybir.dt.float32
BF16 = mybir.dt.bfloat16
AF = mybir.ActivationFunctionType
ALU = mybir.AluOpType
AX = mybir.AxisListType


class Res:
    __slots__ = ("name", "w", "r", "dsem", "dcnt")

    def __init__(self, name):
        self.name = name
        self.w = None
        self.r = {}
        self.dsem = None
        self.dcnt = 0


class Sched:
    def __init__(self, nc, stack):
        self.nc = nc
        self.stack = stack
        self.eng = {"pe": nc.tensor, "act": nc.scalar, "dve": nc.vector,
                    "pool": nc.gpsimd, "sp": nc.sync}
        self.sem = {}
        self.cnt = {}
        self.seen = {}
        for k in self.eng:
            self.sem[k] = stack.enter_context(nc.semaphore("s_" + k))
            self.cnt[k] = 0
            self.seen[k] = {}
        self.nres = 0

    def res(self, name=None):
        self.nres += 1
        return Res(name or f"r{self.nres}")

    def dma_sem(self, r):
        if r.dsem is None:
            self.nres += 1
            r.dsem = self.stack.enter_context(self.nc.semaphore(f"d{self.nres}_" + r.name))
        return r.dsem

    def _wait(self, e, dep):
        kind, who, cnt = dep
        key = who if kind == "e" else ("d", id(who))
        if kind == "e" and who == e and e == "pe":
            return
        if self.seen[e].get(key, 0) >= cnt:
            return
        self.seen[e][key] = cnt
        sem = self.sem[who] if kind == "e" else who.dsem
        self.eng[e].wait_ge(sem, cnt)

    def _deps(self, e, reads, writes):
        for r in reads:
            if r.w is not None:
                self._wait(e, r.w)
        for w in writes:
            if w.w is not None:
                self._wait(e, w.w)
            for k, c in w.r.items():
                if isinstance(k, tuple):
                    self._wait(e, ("d", k[1], c))
                else:
                    self._wait(e, ("e", k, c))

    def op(self, e, fn, reads=(), writes=(), sig=True):
        self._deps(e, reads, writes)
        ins = fn()
        if sig:
            self.cnt[e] += 1
            ins.then_inc(self.sem[e], 1)
            c = self.cnt[e]
        else:
            c = self.cnt[e] + 1
        for r in reads:
            r.r[e] = c
        for w in writes:
            w.w = ("e", e, c)
            w.r = {}
        return ins

    def dma(self, q, out, in_, reads=(), writes=(), **kw):
        self._deps(q, reads, writes)
        carrier = writes[0] if writes else reads[0]
        sem = self.dma_sem(carrier)
        ins = self.eng[q].dma_start(out=out, in_=in_, **kw)
        ins.then_inc(sem, 16)
        carrier.dcnt += 16
        c = carrier.dcnt
        for r in reads:
            r.r[("d", carrier)] = c
        for w in writes:
            w.w = ("d", carrier, c)
            w.r = {}
        return ins

    def barrier_all(self):
        for e in self.eng:
            for o in self.eng:
                if o != e and self.cnt[o] > 0:
                    self._wait(e, ("e", o, self.cnt[o]))

    def finish(self, resources):
        for r in resources:
            if r.w is not None:
                self._wait("sp", r.w)
            for k, c in r.r.items():
                if isinstance(k, tuple):
                    self._wait("sp", ("d", k[1], c))
                else:
                    self._wait("sp", ("e", k, c))

from contextlib import ExitStack
from concourse.bass_utils import run_bass_kernel_spmd

D = 1024
SEQ = 2048
NTT = 16
EPS = 1e-6
NEG = -30000.0
LAM_INIT = 0.8 - 0.6 * 1.0
D_FF = 2816
NFC = 22
ARENA_BASE = 16640
ARENA_END = 229376

PP_N1G, PP_N2G, PP_BG, PP_CW, PP_CB, PP_G = 0, 8, 16, 32, 164, 208
PP_COLS = 212
THIRDS = [(0, 8), (8, 15), (15, 22)]


def na_pairs(qt):
    r = 2 * qt
    if r < 4:
        off = 2 - r // 2
        return [(j, j + off + 1) for j in range(4)]
    if r >= 28:
        off = -((r - 28) // 2)
        return [(12 + j, j + off + 1) for j in range(4)]
    return [((r - 4) // 2 + j, 7 + j) for j in range(5)]


def build_program(nseq, dbg=False):
    nc = bass.Bass("TRN2", target_bir_lowering=False)
    din = lambda n, s: nc.dram_tensor(n, s, F32, kind="ExternalInput").ap()
    x_d = din("x", [nseq, SEQ, D])
    cT_d = din("cT", [128, 8, nseq])
    adaw_d = din("ada_w", [D, 6 * D])
    adab_d = din("ada_b", [1, 6 * D])
    win_d = din("w_in", [D, 3072])
    wna_d = din("w_na_proj", [512, D])
    wdf_d = din("w_df_proj", [512, D])
    wg_d = din("w_gate", [D, 2048])
    wo_d = din("w_out", [D, D])
    wup_d = din("w_up", [D, 2 * D_FF])
    wdn_d = din("w_down", [D_FF, D])
    pp_d = din("pp", [128, PP_COLS])
    lamv_d = din("lamv", [128, 256])
    subln_d = din("subln", [128, 128])
    identf_d = din("identf", [128, 128])
    bdones_d = din("bdones", [128, 128])
    onesel_d = din("onesel", [4, 512])
    diagb_d = din("diagb", [128, 512])
    kaug_d = din("kaug", [8, 8 * SEQ])
    qaug_d = din("qaug", [8, 8, 8 * 256])
    natab_d = din("natab", [8, 128, 1536])
    out_d = nc.dram_tensor("out", [nseq, SEQ, D], F32, kind="ExternalOutput").ap()
    dbg_d = {}

    st = ExitStack()
    S = Sched(nc, st)
    V, A, P, T, G_ = nc.vector, nc.scalar, nc.gpsimd, nc.tensor, None

    cur = [ARENA_BASE]

    def alloc(name, shape, dt, at=None):
        nb = int(np.prod(shape[1:])) * (4 if dt == F32 else 2)
        if at is None:
            off = cur[0]
            cur[0] += (nb + 31) // 32 * 32
            assert cur[0] <= ARENA_END, (name, cur[0])
        else:
            off = at
        return nc.alloc_sbuf_tensor_at(name, list(shape), dt, offset=off)

    PPt = alloc("PPt", [128, PP_COLS], F32)
    IDF = alloc("IDF", [4, 4], F32)
    IDB = alloc("IDB", [128, 128], BF16)
    BD = alloc("BD", [128, 128], BF16)
    ONESEL = alloc("ONESEL", [4, 512], F32)
    DIAGB = alloc("DIAGB", [128, 4, 128], BF16)
    MODT = alloc("MODT", [128, 48, nseq], F32)
    A1 = alloc("A1", [128, 8, nseq], F32)
    A2 = alloc("A2", [128, 8, nseq], F32)
    MODG = alloc("MODG", [4, 2048], F32)
    GBC = alloc("GBC", [128, 2, 1024], F32)
    LAMC = alloc("LAMC", [128, 8], F32)
    SUBLN = alloc("SUBLN", [128, 128], F32)
    STAT = alloc("STAT", [128, 64], F32)
    ONES1 = alloc("ONES1", [1, 4], F32)
    CONS = alloc("CONS", [128, 2], F32)
    WS = [alloc(f"WS{i}", [128, 4096], BF16) for i in range(3)]
    RH = cur[0]; cur[0] += 32768
    RY = cur[0]; cur[0] += 32768
    RM = cur[0]; cur[0] += 32768
    RX = cur[0]; cur[0] += 65536
    assert cur[0] <= ARENA_END, cur[0]

    hT = alloc("hT", [128, 8, SEQ], BF16, at=RH)
    ynaT = alloc("ynaT", [128, 4, SEQ], BF16, at=RY)
    ydfT = alloc("ydfT", [128, 4, SEQ], BF16, at=RY + 16384)
    aT = alloc("aT", [128, 8, SEQ], BF16, at=RY)
    mT = alloc("mT", [128, 8, SEQ], BF16, at=RM)
    x1 = alloc("x1", [128, NTT, D], F32, at=RX)
    XT = [alloc(f"XT{i}", [128, 4, D], F32, at=RX + i * 16384) for i in range(2)]
    XN = [alloc(f"XN{i}", [128, 4, D], BF16, at=RX + 32768 + i * 8192) for i in range(2)]
    JUNK = alloc("JUNK", [128, D], BF16, at=RX + 49152)
    AW = [alloc(f"AW{i}", [128, 8, 512], F32, at=RX + i * 16384) for i in range(2)]
    MODSB = alloc("MODSB", [4, 6 * D], F32, at=RX + 32768)
    ADAB = alloc("ADAB", [1, 6 * D], F32, at=RM)
    CTs = alloc("CTs", [128, 8, nseq], F32, at=RM + 24576)
    SCs = alloc("SCs", [128, 8, nseq], F32, at=RM + 24576 + 256)
    LAMV = alloc("LAMV", [128, 256], F32, at=RM + 24576 + 512)
    LT = alloc("LT", [128, 128], F32, at=RM + 24576 + 2048)
    KN = alloc("KN", [128, 4, SEQ], BF16, at=RX)
    QN = alloc("QN", [128, 4, SEQ], BF16, at=RX + 16384)
    VN = alloc("VN", [128, NTT, 8, 65], BF16, at=RX + 32768)
    TAB = [alloc(f"TAB{i}", [128, 12, 2, 64], F32, at=RX + 49408 + i * 6144) for i in range(2)]
    EN = [alloc(f"EN{i}", [128, 640], BF16, at=RX + 61696 + i * 1280) for i in range(3)]
    ENP = [alloc(f"ENP{i}", [128, 1280], BF16, at=RY + 16384 + i * 2560) for i in range(3)]
    TAB2 = alloc("TAB2", [128, 12, 2, 64], F32, at=RM + 26624)
    TAB3 = alloc("TAB3", [128, 12, 2, 64], F32, at=RY + 16384 + 7680)
    NTAB = [TAB[0], TAB[1], TAB2, TAB3]
    SQ = [alloc(f"SQ{i}", [128, 512], BF16, at=RM + i * 1024) for i in range(2)]
    LNT = [alloc(f"LNT{i}", [128, 512], F32, at=RM + 2048 + i * 2048) for i in range(2)]
    RS = [alloc(f"RS{i}", [128, 512], F32, at=RM + 6144 + i * 2048) for i in range(2)]
    YNATOK = alloc("YNATOK", [128, NTT, 512], BF16, at=RM + 10240)
    RC = alloc("RC", [128, 16], F32, at=RX + 61696)
    KB = alloc("KB", [128, 8, SEQ], BF16, at=RX)
    VD = alloc("VD", [128, NTT, 4, 129], BF16, at=RX + 32768)
    ED = [alloc(f"ED{i}", [128, 4, 256], BF16, at=RX + 49280 + i * 2048) for i in range(3)]
    QB = [alloc(f"QB{i}", [128, 8, 256], BF16, at=RM + 10240 + i * 4096) for i in range(2)]
    O0 = alloc("O0", [128, 2, 128], F32, at=RM + 26688)
    YD = alloc("YD", [128, 2, 4, 128], F32, at=RM + 27712)
    YB = alloc("YB", [128, 2, 512], BF16, at=RX + 55424)
    YSQ = alloc("YSQ", [128, 2, 4, 128], F32, at=RX + 57472)
    GA = [alloc(f"GA{i}", [128, 512], F32, at=RX + i * 2048) for i in range(2)]
    GB = [alloc(f"GB{i}", [128, 512], F32, at=RX + 4096 + i * 2048) for i in range(2)]
    T1 = [alloc(f"T1{i}", [128, 512], F32, at=RX + 8192 + i * 2048) for i in range(2)]
    T2 = [alloc(f"T2{i}", [128, 512], F32, at=RX + 12288 + i * 2048) for i in range(2)]
    XG = [alloc(f"XG{i}", [128, D], F32, at=RY + i * 4096) for i in range(2)]
    TG = [alloc(f"TG{i}", [128, D], F32, at=RY + 8192 + i * 4096) for i in range(2)]
    XNG = alloc("XNG", [128, 4, D], BF16, at=RY + 16384)
    JUNKG = alloc("JUNKG", [128, D], BF16, at=RY + 24576)
    ACC = [alloc(f"ACC{i}", [128, SEQ], F32, at=RM + i * 8192) for i in range(2)]
    GG = alloc("GG", [128, SEQ], F32, at=RM + 16384)
    TD = [alloc(f"TD{i}", [128, D], F32, at=RM + 24576 + i * 4096) for i in range(2)]

    PS = st.enter_context(nc.psum_tensor("PS", [128, 4096], F32))
    bank = lambda i: PS[:, 512 * i:512 * (i + 1)]
    RB = [S.res(f"bank{i}") for i in range(8)]
    RB6b = RB[6]
    RW = [S.res(f"ws{i}") for i in range(3)]
    _rc = {}

    def R(n):
        if n not in _rc:
            _rc[n] = S.res(n)
        return _rc[n]

    def act(out, in_, func, reads, writes, **kw):
        return S.op("act", lambda: A.activation(out=out, in_=in_, func=func, **kw), reads, writes)

    def mm(out, lhsT, rhs, start, stop, reads, writes, sig=None):
        return S.op("pe", lambda: T.matmul(out, lhsT=lhsT, rhs=rhs, start=start, stop=stop),
                    reads, writes, sig=(stop if sig is None else sig))

    def tr(out, in_, reads, writes, sig=True):
        return S.op("pe", lambda: T.transpose(out=out, in_=in_, identity=IDB[:]), reads, writes, sig=sig)

    def dve(fn, reads, writes):
        return S.op("dve", fn, reads, writes)

    def pool(fn, reads, writes):
        return S.op("pool", fn, reads, writes)


    def dump(name, t, shape):
        dd = nc.dram_tensor("dbg_" + name, list(shape), F32, kind="ExternalOutput").ap()
        Rd = R("dbg_" + name)
        S.barrier_all()
        S.dma("pool", dd, t[:], writes=[Rd])
        for e in ("sp", "act", "dve", "pool", "pe"):
            S._deps(e, [Rd], [])
        dbg_d[name] = shape

    NSLOT = 3
    plan = []
    for _b in range(nseq):
        plan += [("win", 512), ("win", 1024), ("win", 0), ("win", 2048), ("win", 2560), ("win", 1536)]
        plan += [("gate", 0), ("proj2", 0), ("gate", 1), ("gate", 2), ("proj2", 1), ("gate", 3)]
        plan += [("win_o", 0), ("win_o", 512)]
        for (_f0, _f1) in THIRDS:
            _fc = _f0
            while _fc < _f1:
                _np = min(2, _f1 - _fc)
                plan.append(("wup", _fc, _np))
                _fc += _np
            for _q in range(0, _f1 - _f0, 4):
                plan.append(("wdn", _f0 + _q, min(4, _f1 - _f0 - _q)))
    wstate = {"issued": 0, "use": 0}
    rr = lambda ap: ap.rearrange("(c p) n -> p c n", p=128)

    def w_issue(k):
        d = plan[k]
        i = k % NSLOT
        rw = [RW[i]]
        if d[0] == "win":
            v = WS[i][:, 0:4096].rearrange("p (c n) -> p c n", c=8)
            S.dma("pool", v, rr(win_d[:, d[1]:d[1] + 512]), writes=rw)
        elif d[0] == "win_o":
            v = WS[i][:, 0:4096].rearrange("p (c n) -> p c n", c=8)
            S.dma("pool", v, rr(wo_d[:, d[1]:d[1] + 512]), writes=rw)
        elif d[0] == "gate":
            ep = d[1]
            v = WS[i][:, 0:4096].rearrange("p (c n) -> p c n", c=8)
            S.dma("pool", v[:, :, 0:256], rr(wg_d[:, ep * 256:(ep + 1) * 256]), writes=rw)
            S.dma("pool", v[:, :, 256:512], rr(wg_d[:, 1024 + ep * 256:1024 + (ep + 1) * 256]), writes=rw)
        elif d[0] == "proj2":
            epp = d[1]
            v = WS[i][:, 0:4096].rearrange("p (a c n) -> p a c n", a=2, c=4)
            S.dma("pool", v[:, 0, :, :], rr(wna_d[:, epp * 512:(epp + 1) * 512]), writes=rw)
            S.dma("pool", v[:, 1, :, :], rr(wdf_d[:, epp * 512:(epp + 1) * 512]), writes=rw)
        elif d[0] == "wup":
            fc, npair = d[1], d[2]
            v = WS[i][:, 0:4096].rearrange("p (c n) -> p c n", c=8)
            S.dma("pool", v[:, :, 0:128 * npair], rr(wup_d[:, fc * 128:(fc + npair) * 128]), writes=rw)
            S.dma("pool", v[:, :, 256:256 + 128 * npair], rr(wup_d[:, D_FF + fc * 128:D_FF + (fc + npair) * 128]), writes=rw)
        elif d[0] == "wdn":
            f, m = d[1], d[2]
            v = WS[i][:, 0:4096].rearrange("p (c n) -> p c n", c=4)
            S.dma("pool", v[:, 0:m, :], rr(wdn_d[f * 128:(f + m) * 128, :]), writes=rw)

    def w_acquire(kind):
        k = wstate["use"]
        wstate["use"] += 1
        assert plan[k][0] == kind, (k, plan[k], kind)
        while wstate["issued"] <= k:
            w_issue(wstate["issued"])
            wstate["issued"] += 1
        i = k % NSLOT
        if kind == "proj2":
            v = WS[i][:, 0:4096].rearrange("p (a c n) -> p a c n", a=2, c=4)
        elif kind == "wdn":
            v = WS[i][:, 0:4096].rearrange("p (c n) -> p c n", c=4)
        else:
            v = WS[i][:, 0:4096].rearrange("p (c n) -> p c n", c=8)
        return k, i, v

    def w_release(k):
        nk = k + NSLOT
        if nk < len(plan) and wstate["issued"] == nk:
            w_issue(nk)
            wstate["issued"] += 1

    Rpp, Ridf, Ridb, Rbd, Rsel, Rdiag, Rmodt, Ra12, Rmodg, Rgbc, Rlam, Rsub, Rstat = [
        R(n) for n in "pp idf idb bd sel diag modt a12 modg gbc lam sub stat".split()]
    Rct, Rsc, Rmodsb, Radab, Rlamv, Rlt, Rones1 = [R(n) for n in "ct sc modsb adab lamv lt ones1".split()]
    Raw = [R("aw0"), R("aw1")]
    S.dma("sp", PPt[:], pp_d, writes=[Rpp])
    S.dma("sp", IDF[:], identf_d[0:4, 0:4], writes=[Ridf])
    S.dma("sp", ONESEL[:], onesel_d, writes=[Rsel])
    S.dma("sp", LAMV[:], lamv_d, writes=[Rlamv])
    S.dma("sp", SUBLN[:], subln_d, writes=[Rsub])
    S.dma("sp", CTs[:], cT_d, writes=[Rct])
    S.dma("sp", ADAB[:], adab_d, writes=[Radab])
    S.dma("pool", IDB[:], identf_d, writes=[Ridb])
    S.dma("pool", BD[:], bdones_d, writes=[Rbd])
    S.dma("pool", DIAGB[:], diagb_d.rearrange("p (h n) -> p h n", h=4), writes=[Rdiag])
    dve(lambda: V.memset(ONES1[:], 1.0), [], [Rones1])
    Rcons = R("cons")
    dve(lambda: V.memset(CONS[:, 0:1], EPS), [], [Rcons])
    S.barrier_all()
    act(SCs[:], CTs[:], AF.Silu, [Rct], [Rsc])
    for ec in range(12):
        S.dma("sp", AW[ec % 2][:], adaw_d[:, ec * 512:(ec + 1) * 512].rearrange("(c p) n -> p c n", p=128),
              writes=[Raw[ec % 2]])
        b = ec % 2
        for dc in range(8):
            mm(bank(b)[0:nseq, :], SCs[:, dc, :], AW[ec % 2][:, dc, :], dc == 0, False,
               [Rsc, Raw[ec % 2]], [RB[b]], sig=False)
        mm(bank(b)[0:nseq, :], ONES1[0:1, 0:nseq], ADAB[0:1, ec * 512:(ec + 1) * 512], False, True,
           [Rones1, Radab], [RB[b]])
        dve(lambda: V.tensor_copy(out=MODSB[0:nseq, ec * 512:(ec + 1) * 512], in_=bank(b)[0:nseq, :]),
            [RB[b]], [Rmodsb])
    for j in range(48):
        mm(bank(2)[:, j * nseq:(j + 1) * nseq], MODSB[0:nseq, j * 128:(j + 1) * 128], IDF[0:nseq, 0:nseq],
           True, True, [Rmodsb, Ridf], [RB[2]], sig=(j == 47))
    dve(lambda: V.tensor_copy(out=MODT[:].rearrange("p a b -> p (a b)"), in_=bank(2)[:, 0:48 * nseq]),
        [RB[2]], [Rmodt])
    for (Ax, sc0, gcol) in ((A1, 8, PP_N1G), (A2, 32, PP_N2G)):
        dve(lambda: V.tensor_scalar(out=Ax[:], in0=MODT[:, sc0:sc0 + 8, :], scalar1=1.0, scalar2=None, op0=ALU.add),
            [Rmodt], [Ra12])
        dve(lambda: V.tensor_tensor(out=Ax[:], in0=Ax[:],
                                    in1=PPt[:, gcol:gcol + 8].unsqueeze(2).to_broadcast([128, 8, nseq]), op=ALU.mult),
            [Ra12, Rpp], [Ra12])
    dve(lambda: V.tensor_copy(out=MODG[0:nseq, 0:1024], in_=MODSB[0:nseq, 2048:3072]), [Rmodsb], [Rmodg])
    dve(lambda: V.tensor_copy(out=MODG[0:nseq, 1024:2048], in_=MODSB[0:nseq, 5120:6144]), [Rmodsb], [Rmodg])
    dve(lambda: V.tensor_tensor(out=LT[:], in0=LAMV[:, 0:128], in1=LAMV[:, 128:256], op=ALU.mult), [Rlamv], [Rlt])
    dve(lambda: V.tensor_reduce(out=LAMC[:, 0:2], in_=LT[:].rearrange("p (a b) -> p a b", a=2), axis=AX.X, op=ALU.add),
        [Rlt], [Rlam])
    act(LAMC[:, 2:4], LAMC[:, 0:2], AF.Exp, [Rlam], [Rlam])
    dve(lambda: V.tensor_tensor(out=LAMC[:, 4:5], in0=LAMC[:, 2:3], in1=LAMC[:, 3:4], op=ALU.subtract), [Rlam], [Rlam])
    dve(lambda: V.tensor_scalar(out=LAMC[:, 5:6], in0=LAMC[:, 4:5], scalar1=-1.0, scalar2=-LAM_INIT,
                                op0=ALU.mult, op1=ALU.add), [Rlam], [Rlam])
    dve(lambda: V.tensor_scalar(out=SUBLN[:], in0=SUBLN[:], scalar1=1.0 - LAM_INIT, scalar2=None, op0=ALU.mult),
        [Rsub], [Rsub])
    S.barrier_all()

    for _k in range(NSLOT):
        w_issue(_k)
    wstate["issued"] = NSLOT

    gcolAP = lambda k, p0, p1: PPt[p0:p1, PP_G + k:PP_G + k + 1]
    rot = {"sq": 0, "bk": 0}

    def qknorm_a(src_ps, Rsrc, Pn, N):
        i = rot["sq"] % 2
        rot["sq"] += 1
        Rsq, Rln, Rrs = qk_res[i]
        act(SQ[i][0:Pn, 0:N], src_ps, AF.Square, [Rsrc], [Rsq])
        return i

    def qknorm_b(i, src_ps, Rsrc, Pn, N, gk, dst_list, Rdst, ssbank, Rss):
        Rsq, Rln, Rrs = qk_res[i]
        mm(ssbank[0:Pn, 0:N], BD[0:Pn, 0:Pn], SQ[i][0:Pn, 0:N], True, True, [Rbd, Rsq], [Rss])
        act(LNT[i][0:Pn, 0:N], ssbank[0:Pn, 0:N], AF.Ln, [Rss, Rcons], [Rln], scale=1.0 / 64, bias=CONS[0:Pn, 0:1])
        act(RS[i][0:Pn, 0:N], LNT[i][0:Pn, 0:N], AF.Exp, [Rln], [Rrs], scale=-0.5)
        for (dst, p0, p1) in dst_list:
            dve(lambda: V.scalar_tensor_tensor(out=dst, in0=src_ps[p0:p1, :], scalar=gcolAP(gk, p0, p1),
                                               in1=RS[i][p0:p1, 0:N], op0=ALU.mult, op1=ALU.mult),
                [Rsrc, Rrs, Rpp], [Rdst])

    def qknorm(src_ps, Rsrc, Pn, N, gk, dst_list, Rdst, ssbank, Rss):
        i = qknorm_a(src_ps, Rsrc, Pn, N)
        qknorm_b(i, src_ps, Rsrc, Pn, N, gk, dst_list, Rdst, ssbank, Rss)

    qk_res = [(R("sq0"), R("ln0"), R("rs0")), (R("sq1"), R("ln1"), R("rs1"))]

    def norm_to_T(b, srcs, Rsrcs, Acoef, Bcol0, dstT, RdstT, XNbuf, Rxn, junk, Rjunk, tq, pbanks, al_xn=(), al_junk=(), part=0):
        t0_ = 4 * tq
        do_stats = part in (0, 1, 11)
        do_xn = part in (0, 1, 12)
        do_tr = part in (0, 2, 22)
        for i in range(4 if do_stats else 0):
            tt = t0_ + i
            act(junk[:], srcs[i], AF.Square, [Rsrcs[i]], [Rjunk, Rstat] + list(al_junk), accum_out=STAT[:, tt:tt + 1])
        if do_stats:
            act(STAT[:, 16 + t0_:20 + t0_], STAT[:, t0_:t0_ + 4], AF.Ln, [Rstat, Rcons], [Rstat], scale=1.0 / D, bias=CONS[:, 0:1])
            act(STAT[:, 32 + t0_:36 + t0_], STAT[:, 16 + t0_:20 + t0_], AF.Exp, [Rstat], [Rstat], scale=-0.5)
        for i in range(4 if do_xn else 0):
            tt = t0_ + i
            dve(lambda: V.tensor_scalar(out=XNbuf[:, i, :], in0=srcs[i], scalar1=STAT[:, 32 + tt:33 + tt], scalar2=None,
                                        op0=ALU.mult), [Rsrcs[i], Rstat], [Rxn] + list(al_xn))
        for c in range(8 if do_tr else 0):
            pb = pbanks[c % len(pbanks)]
            ptv = bank(pb).bitcast(BF16)
            for i in range(4):
                tr(ptv[:, i * 128:(i + 1) * 128], XNbuf[:, i, c * 128:(c + 1) * 128], [Rxn, Ridb], [RB[pb]], sig=(i == 3))
            dst = dstT[:, c, tq * 512:(tq + 1) * 512]
            sc = Acoef[:, c, b:b + 1]
            bi = MODT[:, Bcol0 + c, b:b + 1]
            if c % 2 == 0 and part != 22:
                act(dst, ptv[:, 0:512], AF.Identity, [RB[pb], Ra12, Rmodt], [RdstT[tq][c]], scale=sc, bias=bi)
            else:
                dve(lambda: V.tensor_scalar(out=dst, in0=ptv[:, 0:512], scalar1=sc, scalar2=bi, op0=ALU.mult, op1=ALU.add),
                    [RB[pb], Ra12, Rmodt], [RdstT[tq][c]])

    out_res = []
    RX1g = [R(f"x1_{t}") for t in range(NTT)]
    for b in range(nseq):
        RhT = [[R(f"hT{t}_{c}") for c in range(8)] for t in range(4)]
        RhT_all = [r for l in RhT for r in l]
        for gi in range(2):
            for hf in range(2):
                bk = 6 + hf
                mm(bank(bk), ONESEL[0:nseq, b * 128:(b + 1) * 128], MODG[0:nseq, gi * 1024 + hf * 512: gi * 1024 + (hf + 1) * 512],
                   True, True, [Rsel, Rmodg], [RB[bk]])
                dve(lambda: V.tensor_copy(out=GBC[:, gi, hf * 512:(hf + 1) * 512], in_=bank(bk)), [RB[bk]], [Rgbc])
        Rxt = [[R(f"xt{i}_{k}") for k in range(4)] for i in range(2)]
        Rxn = [R("xn0"), R("xn1")]
        Rjunk = R("junk")
        def phA(tq, part):
            bi = tq % 2
            if part in (1, 11):
                for i in range(4):
                    tt = 4 * tq + i
                    S.dma("sp", XT[bi][:, i, :], x_d[b, tt * 128:(tt + 1) * 128, :], writes=[Rxt[bi][i], RX1g[4 * bi + i]])
            norm_to_T(b, [XT[bi][:, i, :] for i in range(4)], Rxt[bi], A1, 0, hT, RhT, XN[bi], Rxn[bi], JUNK, Rjunk, tq, [0, 1],
                      al_xn=[RX1g[8 + 2 * bi], RX1g[9 + 2 * bi]], al_junk=RX1g[12:16], part=part)

        phA(0, 1)
        phA(1, 11)
        phA(1, 12)
        for t_ in range(3):
            if t_ + 2 < 4:
                phA(t_ + 2, 11)
            phA(t_, 22)
            if t_ + 2 < 4:
                phA(t_ + 2, 12)
        phA(3, 22)
        S.barrier_all()
        if dbg and b == 0:
            dump("hT", hT, [128, 8, SEQ])

        RKN = [[R(f"kn{c}_{t}") for t in range(4)] for c in range(4)]
        RQN = [[R(f"qn{c}_{t}") for t in range(4)] for c in range(4)]
        RVN = [R(f"vn{t}") for t in range(NTT)]
        Rvn1 = R("vnones")
        dve(lambda: V.memset(VN[:, :, :, 64:65], 1.0), [], [Rvn1])
        for (dstT_, Rd, c0, gk) in ((KN, RKN, 512, 1), (None, None, 1024, None), (QN, RQN, 0, 0)):
            wk, si, Wv = w_acquire("win")
            if dstT_ is None:
                for tt in range(NTT):
                    bk = 4 + tt % 2
                    for dc in range(8):
                        mm(bank(bk), hT[:, dc, tt * 128:(tt + 1) * 128], Wv[:, dc, :], dc == 0, dc == 7,
                           [RW[si]] + ([r for l in RhT for r in l] if dc == 0 else []), [RB[bk]])
                    dstv = VN[:, tt, :, 0:64]
                    srcv = bank(bk).rearrange("p (h d) -> p h d", h=8)
                    if tt % 2 == 0:
                        act(dstv, srcv, AF.Copy, [RB[bk]], [RVN[tt]])
                    else:
                        dve(lambda: V.tensor_copy(out=dstv, in_=srcv), [RB[bk]], [RVN[tt]])
            else:
                for ch in range(4):
                    for tq in range(4):
                        bk = rot["bk"] % 2
                        rot["bk"] += 1
                        for dc in range(8):
                            mm(bank(bk), Wv[:, dc, ch * 128:(ch + 1) * 128], hT[:, dc, tq * 512:(tq + 1) * 512], dc == 0, dc == 7,
                               [RW[si]] + (RhT[tq] if dc == 0 else []), [RB[bk]])
                        dst = dstT_[:, ch, tq * 512:(tq + 1) * 512]
                        qknorm(bank(bk), RB[bk], 128, 512, gk, [(dst, 0, 128)], Rd[ch][tq], bank(2 + bk), RB[2 + bk])
            w_release(wk)
        if dbg and b == 0:
            dump("KN", KN, [128, 4, SEQ])
            dump("QN", QN, [128, 4, SEQ])
            dump("VN", VN, [128, NTT, 8, 65])

        Rtab = [R(f"tab{i}") for i in range(4)]
        Ren = [R(f"en{i}") for i in range(3)]
        Rytok = [R(f"ytok{t}") for t in range(NTT)]
        Rrc = R("rc")
        RYna = [R(f"yna{t}") for t in range(NTT)]
        RKN_all = [r for l in RKN for r in l]
        RQN_all = [r for l in RQN for r in l]
        Rrcs = [R("rcA"), R("rcB")]

        def na_front(it, ch, qt):
            pairs = na_pairs(qt)
            n = len(pairs)
            sb = it % 2
            eb = it % 3
            base = 1536 * sb
            Rs_ = [RB[3 * sb], RB[3 * sb + 1], RB[3 * sb + 2]]
            for j, (vt, slot) in enumerate(pairs):
                for hh in range(2):
                    p0, p1 = 64 * hh, 64 * hh + 64
                    o = base + 640 * hh + j * 128
                    mm(PS[:, o:o + 128], KN[p0:p1, ch, vt * 128:(vt + 1) * 128],
                       QN[p0:p1, ch, qt * 128:(qt + 1) * 128], True, True,
                       (RKN_all + RQN_all) if (j == 0 and hh == 0 and it == 0) else [], Rs_, sig=(j == n - 1 and hh == 1))
            s0 = pairs[0][1]
            for hh in range(2):
                tb = 2 * (ch % 2) + hh
                Sv4 = PS[:, base + 640 * hh:base + 640 * hh + n * 128].rearrange("p (j a c) -> p j a c", j=n, a=2)
                dve(lambda: V.scalar_tensor_tensor(out=Sv4, in0=Sv4, scalar=0.125, in1=NTAB[tb][:, s0:s0 + n, :, :],
                                                   op0=ALU.mult, op1=ALU.add), Rs_ + [Rtab[tb]], Rs_)
            w = 640 + n * 128
            act(ENP[eb][:, 0:w], PS[:, base:base + w], AF.Exp, Rs_, [Ren[eb]])

        def na_back(it, ch, qt):
            pairs = na_pairs(qt)
            n = len(pairs)
            eb = it % 3
            pvb = 6 + it % 2
            ri = it % 2
            for hh in range(2):
                h = 2 * ch + hh
                for j, (vt, slot) in enumerate(pairs):
                    S.op("pe", lambda: T.matmul(bank(pvb)[:, 65 * hh:65 * hh + 65], lhsT=ENP[eb][:, 640 * hh + j * 128:640 * hh + (j + 1) * 128],
                                                rhs=VN[:, vt, h, :], start=(j == 0 and hh == 0), stop=(j == n - 1 and hh == 1),
                                                skip_group_check=True),
                         [Ren[eb], Rvn1] + (RVN if (j == 0 and hh == 0 and it == 0) else []), [RB[pvb]],
                         sig=(j == n - 1 and hh == 1))
            pov = bank(pvb)[:, 0:130].rearrange("p (a n) -> p a n", a=2)
            dve(lambda: V.reciprocal(out=RC[:, 2 * ri:2 * ri + 2], in_=pov[:, :, 64]), [RB[pvb]], [Rrcs[ri]])
            for hh in range(2):
                h = 2 * ch + hh
                act(YNATOK[:, qt, h * 64:(h + 1) * 64], pov[:, hh, 0:64], AF.Copy, [RB[pvb], Rrcs[ri]], [Rytok[qt]],
                    scale=RC[:, 2 * ri + hh:2 * ri + hh + 1])

        items = [(ch, qt) for ch in range(4) for qt in range(NTT)]

        def load_tabs(ch):
            for hh in range(2):
                tb = 2 * (ch % 2) + hh
                S.dma("sp", NTAB[tb][:].rearrange("p a b c -> p (a b c)"), natab_d[2 * ch + hh], writes=[Rtab[tb]])

        load_tabs(0)
        for it, (ch, qt) in enumerate(items):
            if qt == 0 and ch + 1 < 4:
                load_tabs(ch + 1)
            na_front(it, ch, qt)
            if it >= 2:
                na_back(it - 2, *items[it - 2])
        na_back(len(items) - 2, *items[-2])
        na_back(len(items) - 1, *items[-1])
        for tt in range(NTT):
            pb = 6 + tt % 2
            ptv = bank(pb).bitcast(BF16)
            for cc in range(4):
                tr(ptv[:, cc * 128:(cc + 1) * 128], YNATOK[:, tt, cc * 128:(cc + 1) * 128], [Rytok[tt], Ridb], [RB[pb]], sig=(cc == 3))
            dve(lambda: V.tensor_copy(out=ynaT[:, :, tt * 128:(tt + 1) * 128], in_=ptv[:, 0:512].rearrange("p (c t) -> p c t", c=4)),
                [RB[pb]], [RYna[tt]])
        S.barrier_all()
        if dbg and b == 0:
            dump("ynaT", ynaT, [128, 4, SEQ])

        RKB = [R(f"kb{i}") for i in range(4)]
        Rkaug = R("kaug")
        RVD = [R(f"vd{t}") for t in range(NTT)]
        Rvd1 = R("vdones")
        S.dma("pool", KB[64:72, :, :], kaug_d.rearrange("r (a n) -> r a n", a=8), writes=[Rkaug])
        dve(lambda: V.memset(VD[:, :, :, 128:129], 1.0), [], [Rvd1])
        wk, si, Wv = w_acquire("win")
        for hp in range(4):
            for tq in range(4):
                bk = rot["bk"] % 2
                rot["bk"] += 1
                for dc in range(8):
                    mm(bank(bk), Wv[:, dc, hp * 128:(hp + 1) * 128], hT[:, dc, tq * 512:(tq + 1) * 512], dc == 0, dc == 7,
                       [RW[si]], [RB[bk]])
                tok = slice(tq * 512, (tq + 1) * 512)
                qknorm(bank(bk), RB[bk], 128, 512, 3,
                       [(KB[0:64, 2 * hp, tok], 0, 64), (KB[0:64, 2 * hp + 1, tok], 64, 128)], RKB[hp],
                       bank(2 + bk), RB[2 + bk])
        w_release(wk)
        wk, si, Wv = w_acquire("win")
        for tt in range(NTT):
            bk = 4 + tt % 2
            for dc in range(8):
                mm(bank(bk), hT[:, dc, tt * 128:(tt + 1) * 128], Wv[:, dc, :], dc == 0, dc == 7, [RW[si]], [RB[bk]])
            dstv = VD[:, tt, :, 0:128]
            srcv = bank(bk).rearrange("p (h d) -> p h d", h=4)
            if tt % 2 == 0:
                act(dstv, srcv, AF.Copy, [RB[bk]], [RVD[tt]])
            else:
                dve(lambda: V.tensor_copy(out=dstv, in_=srcv), [RB[bk]], [RVD[tt]])
        w_release(wk)
        wkq, siq, WQ = w_acquire("win")
        if dbg and b == 0:
            dump("KB", KB, [128, 8, SEQ])
            dump("VD", VD, [128, NTT, 4, 129])

        RQB = [[R(f"qb{s}_{i}") for i in range(4)] for s in range(2)]
        RQaug = [R("qaug0"), R("qaug1")]
        Red = [R(f"ed{i}") for i in range(3)]
        Ro0, Ryd, Ryb, Rysq = R("o0"), R("yd"), R("yb"), R("ysq")
        Rrct = R("rct")
        RYdf = [R(f"ydf{t}") for t in range(NTT)]

        def q_aug(jt):
            qs = jt % 2
            S.dma("pool", QB[qs][64:72, :, :], qaug_d[jt].rearrange("r (a n) -> r a n", a=8), writes=[RQaug[qs]])

        qst = {}

        def q_proj_a(jt, hp):
            for dc in range(8):
                mm(bank(6)[:, 0:256], WQ[:, dc, hp * 128:(hp + 1) * 128], hT[:, dc, jt * 256:(jt + 1) * 256], dc == 0, dc == 7,
                   [RW[siq]], [RB[6]])
            qst[(jt, hp)] = qknorm_a(bank(6)[:, 0:256], RB[6], 128, 256)

        def q_proj_b(jt, hp):
            qs = jt % 2
            qknorm_b(qst.pop((jt, hp)), bank(6)[:, 0:256], RB[6], 128, 256, 2,
                     [(QB[qs][0:64, 2 * hp, :], 0, 64), (QB[qs][0:64, 2 * hp + 1, :], 64, 128)], RQB[qs][hp],
                     bank(6)[:, 256:512], RB6b)

        def q_proj(jt, hp):
            q_proj_a(jt, hp)
            q_proj_b(jt, hp)

        def d_front(gi, jt, h, c, g):
            qs = jt % 2
            hc = 2 * h + c
            sb = gi % 2
            eb = gi % 3
            Rs_ = [RB[2 * sb], RB[2 * sb + 1]]
            base = 1024 * sb
            rq = [RQB[qs][h], RQaug[qs], RKB[h], Rkaug]
            for kk in range(4):
                kt = 4 * g + kk
                o = base + kk * 256
                last = (kk == 3)
                kslice = slice(kt * 128, (kt + 1) * 128)
                if kt < 2 * jt:
                    mm(PS[:, o:o + 256], KB[0:68, hc, kslice], QB[qs][0:68, hc, :], True, True, rq, Rs_, sig=last)
                elif kt > 2 * jt + 1:
                    mm(PS[:, o:o + 256], KB[0:72, hc, kslice], QB[qs][0:72, hc, :], True, True, rq, Rs_, sig=last)
                else:
                    e = kt - 2 * jt
                    oth = 1 - e
                    kr = 68 if e == 0 else 72
                    mm(PS[:, o + oth * 128:o + (oth + 1) * 128], KB[0:kr, hc, kslice],
                       QB[qs][0:kr, hc, oth * 128:(oth + 1) * 128], True, True, rq, Rs_, sig=False)
                    mm(PS[:, o + e * 128:o + (e + 1) * 128], KB[0:64, hc, kslice],
                       QB[qs][0:64, hc, e * 128:(e + 1) * 128], True, False, rq, Rs_, sig=False)
                    mm(PS[:, o + e * 128:o + (e + 1) * 128], IDB[:, :], DIAGB[:, h, :], False, True,
                       [Ridb, Rdiag], Rs_, sig=last)
            act(ED[eb][:].rearrange("p a b -> p (a b)"), PS[:, base:base + 1024], AF.Exp, Rs_, [Red[eb]], scale=0.125)

        def d_back(gi, jt, h, c, g):
            eb = gi % 3
            accb = 4 + c
            for kk in range(4):
                kt = 4 * g + kk
                for s_ in range(2):
                    S.op("pe", lambda: T.matmul(bank(accb)[:, s_ * 129:(s_ + 1) * 129], lhsT=ED[eb][:, kk, s_ * 128:(s_ + 1) * 128],
                                                rhs=VD[:, kt, h, :], start=(kt == 0 and s_ == 0), stop=(kt == 15 and s_ == 1),
                                                skip_group_check=True),
                         [Red[eb], RVD[kt], Rvd1], [RB[accb]], sig=(kk == 3 and s_ == 1))
            if g != 3:
                return
            accv = bank(accb)[:, 0:258].rearrange("p (s n) -> p s n", s=2)
            if c == 0:
                dve(lambda: V.reciprocal(out=RC[:, 2:4], in_=accv[:, :, 128]), [RB[accb]], [Rrc])
                for s_ in range(2):
                    dve(lambda: V.tensor_scalar(out=O0[:, s_, :], in0=accv[:, s_, 0:128], scalar1=RC[:, 2 + s_:3 + s_],
                                                scalar2=None, op0=ALU.mult), [RB[accb], Rrc], [Ro0])
            else:
                dve(lambda: V.reciprocal(out=RC[:, 4:6], in_=accv[:, :, 128]), [RB[accb]], [Rrc])
                dve(lambda: V.tensor_scalar(out=RC[:, 6:8], in0=RC[:, 4:6], scalar1=LAMC[:, 5:6], scalar2=None,
                                            op0=ALU.mult), [Rrc, Rlam], [Rrc])
                for s_ in range(2):
                    dve(lambda: V.scalar_tensor_tensor(out=YD[:, s_, h, :], in0=accv[:, s_, 0:128], scalar=RC[:, 6 + s_:7 + s_],
                                                       in1=O0[:, s_, :], op0=ALU.mult, op1=ALU.add),
                        [RB[accb], Rrc, Ro0], [Ryd])

        def d_tail_1(jt):
            ydf = YD[:].rearrange("p s h n -> p (s h) n")
            ysf = YSQ[:].rearrange("p s h n -> p (s h) n")
            dve(lambda: V.tensor_tensor(out=ysf, in0=ydf, in1=ydf, op=ALU.mult), [Ryd], [Rysq])
            dve(lambda: V.tensor_reduce(out=RC[:, 8:16], in_=ysf, axis=AX.X, op=ALU.add), [Rysq], [Rrct])

        def d_tail_2(jt):
            act(RC[:, 8:16], RC[:, 8:16], AF.Ln, [Rrct, Rcons], [Rrct], scale=1.0 / 128, bias=CONS[:, 0:1])
            act(RC[:, 8:16], RC[:, 8:16], AF.Exp, [Rrct], [Rrct], scale=-0.5)

        def d_tail_3(jt):
            ydf = YD[:].rearrange("p s h n -> p (s h) n")
            ysf = YSQ[:].rearrange("p s h n -> p (s h) n")
            dve(lambda: V.tensor_tensor(out=ysf, in0=ydf, in1=RC[:, 8:16].unsqueeze(2).to_broadcast([128, 8, 128]), op=ALU.mult),
                [Ryd, Rrct], [Rysq])
            dve(lambda: V.tensor_tensor(out=YB[:].rearrange("p s (h n) -> p (s h) n", h=4), in0=ysf,
                                        in1=SUBLN[:].unsqueeze(1).to_broadcast([128, 8, 128]), op=ALU.mult),
                [Rysq, Rsub], [Ryb])

        def d_tail_b(jt, halves=(0, 1)):
            for s_ in halves:
                tt = 2 * jt + s_
                ptv = bank(7).bitcast(BF16)
                for cc in range(4):
                    tr(ptv[:, cc * 128:(cc + 1) * 128], YB[:, s_, cc * 128:(cc + 1) * 128], [Ryb, Ridb], [RB[7]], sig=(cc == 3))
                dve(lambda: V.tensor_copy(out=ydfT[:, :, tt * 128:(tt + 1) * 128], in_=ptv[:, 0:512].rearrange("p (c t) -> p c t", c=4)),
                    [RB[7]], [RYdf[tt]])

        q_aug(0)
        for hp in range(4):
            q_proj(0, hp)
        groups = [(jt, h, c, g) for jt in range(8) for h in range(4) for c in range(2) for g in range(4)]
        for gi, (jt, h, c, g) in enumerate(groups):
            li = gi % 32
            if li == 0 and jt + 1 < 8:
                q_aug(jt + 1)
            d_front(gi, jt, h, c, g)
            if gi >= 2:
                pj = groups[gi - 2]
                d_back(gi - 2, *pj)
                if (gi - 2) % 32 == 31:
                    d_tail_1(pj[0])
            if jt >= 1:
                if li == 4:
                    d_tail_2(jt - 1)
                if li == 6:
                    d_tail_3(jt - 1)
                if li == 9:
                    d_tail_b(jt - 1, (0,))
                if li == 11:
                    d_tail_b(jt - 1, (1,))
            if jt + 1 < 8 and li in (5, 13, 21, 29):
                q_proj_a(jt + 1, (li - 5) // 8)
            if jt + 1 < 8 and li in (7, 15, 23, 31):
                q_proj_b(jt + 1, (li - 7) // 8)
        d_back(len(groups) - 2, *groups[-2])
        d_back(len(groups) - 1, *groups[-1])
        d_tail_1(7)
        d_tail_2(7)
        d_tail_3(7)
        hookE = (lambda: d_tail_b(7))
        if dbg:
            hookE()
            hookE = None
        w_release(wkq)
        if dbg and b == 0:
            dump("ydfT", ydfT, [128, 4, SEQ])

        RmT = [[R(f"mT{e}_{t}") for t in range(4)] for e in range(8)]
        Rga, Rgb, Rt1, Rt2 = [[R(f"{n}{i}") for i in range(2)] for n in ("ga", "gb", "t1", "t2")]
        fit = 0
        for ep in range(4):
            wkg, sg, Wg = w_acquire("gate")
            if ep % 2 == 0:
                wkp, sp_, Wp = w_acquire("proj2")
            pc0 = (ep % 2) * 256
            def f_gate(e2, tq, i):
                b0 = 4 * i
                tok = slice(tq * 512, (tq + 1) * 512)
                for dc in range(8):
                    mm(bank(b0), Wg[:, dc, e2 * 128:(e2 + 1) * 128], hT[:, dc, tok], dc == 0, dc == 7, [RW[sg]], [RB[b0]])
                for dc in range(8):
                    mm(bank(b0 + 1), Wg[:, dc, 256 + e2 * 128:256 + (e2 + 1) * 128], hT[:, dc, tok], dc == 0, dc == 7, [RW[sg]], [RB[b0 + 1]])

            def f_proj(e2, tq, i):
                e = 2 * ep + e2
                b0 = 4 * i
                tok = slice(tq * 512, (tq + 1) * 512)
                for cc in range(4):
                    mm(bank(b0 + 2), Wp[:, 0, cc, pc0 + e2 * 128:pc0 + (e2 + 1) * 128], ynaT[:, cc, tok], cc == 0, cc == 3,
                       [RW[sp_]] + RYna[4 * tq:4 * tq + 4], [RB[b0 + 2]])
                for cc in range(4):
                    mm(bank(b0 + 3), Wp[:, 1, cc, pc0 + e2 * 128:pc0 + (e2 + 1) * 128], ydfT[:, cc, tok], cc == 0, cc == 3,
                       [RW[sp_]] + RYdf[4 * tq:4 * tq + 4], [RB[b0 + 3]])
                act(GA[i][:], bank(b0), AF.Sigmoid, [RB[b0], Rpp], [Rga[i]], bias=PPt[:, PP_BG + e:PP_BG + e + 1])
                act(GB[i][:], bank(b0 + 1), AF.Sigmoid, [RB[b0 + 1], Rpp], [Rgb[i]], bias=PPt[:, PP_BG + 8 + e:PP_BG + 9 + e])
                dve(lambda: V.tensor_tensor(out=T1[i][:], in0=bank(b0 + 2), in1=GA[i][:], op=ALU.mult), [RB[b0 + 2], Rga[i]], [Rt1[i]])
                dve(lambda: V.tensor_tensor(out=T2[i][:], in0=bank(b0 + 3), in1=GB[i][:], op=ALU.mult), [RB[b0 + 3], Rgb[i]], [Rt2[i]])
                dve(lambda: V.tensor_tensor(out=mT[:, e, tok], in0=T1[i][:], in1=T2[i][:], op=ALU.add), [Rt1[i], Rt2[i]], [RmT[e][tq]])

            blks = [(e2, tq) for e2 in range(2) for tq in range(4)]
            bi_ = 0
            while bi_ < 8:
                if fit == 2 and hookE is not None:
                    hookE()
                    hookE = None
                if ep == 2 and bi_ == 0:
                    i0, i1 = fit % 2, (fit + 1) % 2
                    f_gate(*blks[0], i0)
                    f_gate(*blks[1], i1)
                    f_proj(*blks[0], i0)
                    f_proj(*blks[1], i1)
                    fit += 2
                    bi_ += 2
                else:
                    i0 = fit % 2
                    f_gate(*blks[bi_], i0)
                    f_proj(*blks[bi_], i0)
                    fit += 1
                    bi_ += 1
            if ep % 2 == 0:
                w_release(wkg)
            else:
                w_release(wkp)
                w_release(wkg)
        if dbg and b == 0:
            dump("mT", mT, [128, 8, SEQ])

        RX1 = [R(f"x1_{t}") for t in range(NTT)]
        Rh2T = [[R(f"hT{t}_{c}") for c in range(8)] for t in range(4)]
        Rxg, Rtg = [R("xg0"), R("xg1")], [R("tg0"), R("tg1")]
        Rxng, Rjg = R("xng"), R("junkg")
        _a0 = w_acquire("win_o")
        _a1 = w_acquire("win_o")
        so = [_a0[1], _a1[1]]
        Wo = [_a0[2], _a1[2]]
        RmT_all = [r for l in RmT for r in l]
        def phG_tile_half(tt, hf):
            xi = tt % 2
            bk = 2 * (tt % 2) + hf
            tq = tt // 4
            for dc in range(8):
                mm(bank(bk), mT[:, dc, tt * 128:(tt + 1) * 128], Wo[hf][:, dc, :], dc == 0, dc == 7,
                   [RW[so[hf]]] + ([RmT[dc][tq]]), [RB[bk]])
            cs = slice(hf * 512, (hf + 1) * 512)
            dve(lambda: V.tensor_tensor(out=TG[xi][:, cs], in0=bank(bk), in1=GBC[:, 0, cs], op=ALU.mult),
                [RB[bk], Rgbc], [Rtg[xi]])

        def phG_load(tt):
            xi = tt % 2
            S.dma("sp", XG[xi][:], x_d[b, tt * 128:(tt + 1) * 128, :], writes=[Rxg[xi]] + (RYna if tt < 2 else []))

        def phG_add(tt):
            xi = tt % 2
            dve(lambda: V.tensor_tensor(out=x1[:, tt, :], in0=TG[xi][:], in1=XG[xi][:], op=ALU.add),
                [Rtg[xi], Rxg[xi]], [RX1[tt]])

        def phG_mm(tq):
            if tq == 0:
                for p_ in range(2):
                    tA, tB = 2 * p_, 2 * p_ + 1
                    phG_load(tA)
                    phG_load(tB)
                    for hf in range(2):
                        phG_tile_half(tA, hf)
                        phG_tile_half(tB, hf)
                    phG_add(tA)
                    phG_add(tB)
                return
            for i in range(4):
                tt = 4 * tq + i
                phG_load(tt)
                for hf in range(2):
                    phG_tile_half(tt, hf)
                phG_add(tt)

        def phG_n(tq, part):
            norm_to_T(b, [x1[:, 4 * tq + i, :] for i in range(4)], RX1[4 * tq:4 * tq + 4], A2, 24, hT, Rh2T, XNG, Rxng,
                      JUNKG, Rjg, tq, [4, 5], part=part)

        phG_mm(0)
        phG_n(0, 1)
        for tq in range(1, 4):
            phG_mm(tq)
            phG_n(tq - 1, 2)
            phG_n(tq, 1)
        hookG = (lambda: phG_n(3, 2))
        if dbg:
            hookG()
            hookG = None
        w_release(_a0[0])
        w_release(_a1[0])
        if dbg and b == 0:
            dump("x1", x1, [128, NTT, D])
            dump("h2T", hT, [128, 8, SEQ])

        Racc = [[R("acc0a"), R("acc0b")], [R("acc1a"), R("acc1b")]]
        Rgg = [R("gga"), R("ggb")]
        Rtd = [R("td0"), R("td1")]
        H2 = SEQ // 2
        for (f0, f1) in THIRDS:
            RaT = [[R(f"aT{i}a"), R(f"aT{i}b")] for i in range(f1 - f0)]
            fc = f0
            while fc < f1:
                npair = min(2, f1 - fc)
                wku, su, Wu = w_acquire("wup")
                for k in range(npair):
                    f = fc + k
                    for half in range(2):
                        acc, Ra = ACC[half], Racc[half]
                        pb0 = 4 * half
                        Rp = RB[pb0:pb0 + 4]
                        col0 = half * 256 + k * 128
                        for tq in range(4):
                            for dc in range(8):
                                mm(bank(pb0 + tq), Wu[:, dc, col0:col0 + 128], hT[:, dc, tq * 512:(tq + 1) * 512], dc == 0, dc == 7,
                                   [RW[su]] + (Rh2T[tq] if dc == 0 else []), [Rp[tq]])
                            if tq == 2 and hookG is not None:
                                hookG()
                                hookG = None
                        U = PS[:, 2048 * half:2048 * (half + 1)]
                        fcol = half * NFC + f
                        w0 = PPt[:, PP_CW + fcol:PP_CW + fcol + 1]
                        w1 = PPt[:, PP_CW + 44 + fcol:PP_CW + 45 + fcol]
                        w2 = PPt[:, PP_CW + 88 + fcol:PP_CW + 89 + fcol]
                        cb = PPt[:, PP_CB + fcol:PP_CB + fcol + 1]
                        if f == f1 - 1:
                            segs = [(0, H2, [Ra[0]], [Rgg[0]], [RaT[f - f0][0]], Rp[0:3], [Ra[1]]),
                                    (H2, SEQ, [Ra[1]], [Rgg[1]], [RaT[f - f0][1]], Rp[1:4], [])]
                        else:
                            segs = [(0, SEQ, Ra, Rgg, RaT[f - f0], Rp, [])]
                        for (t0, t1, ra, rg, rat, rp, xr) in segs:
                            act(acc[:, t0:t1], U[:, t0:t1], AF.Identity, rp + [Rpp], ra, scale=w1, bias=cb)
                        for (t0, t1, ra, rg, rat, rp, xr) in segs:
                            lo = max(t0, 1)
                            dve(lambda: V.scalar_tensor_tensor(out=acc[:, lo:t1], in0=U[:, lo - 1:t1 - 1], scalar=w0, in1=acc[:, lo:t1],
                                                               op0=ALU.mult, op1=ALU.add), rp + ra + [Rpp], ra)
                            hi = min(t1, SEQ - 1)
                            dve(lambda: V.scalar_tensor_tensor(out=acc[:, t0:hi], in0=U[:, t0 + 1:hi + 1], scalar=w2, in1=acc[:, t0:hi],
                                                               op0=ALU.mult, op1=ALU.add), rp + ra + xr + [Rpp], ra)
                            if half == 0:
                                act(GG[:, t0:t1], acc[:, t0:t1], AF.Gelu, ra, rg)
                            else:
                                dve(lambda: V.tensor_tensor(out=aT[:, f - f0, t0:t1], in0=acc[:, t0:t1], in1=GG[:, t0:t1], op=ALU.mult),
                                    ra + rg, rat)
                w_release(wku)
                fc += npair
            nch = f1 - f0
            sd = []
            for q in range(0, nch, 4):
                wkd, s_, Wd = w_acquire("wdn")
                sd.append((s_, Wd, wkd))
            for tt in range(NTT):
                ti = tt % 2
                for hf in range(2):
                    bk = 2 * ti + hf
                    for q in range(nch):
                        s_, Wd, _ = sd[q // 4]
                        mm(bank(bk), aT[:, q, tt * 128:(tt + 1) * 128], Wd[:, q % 4, hf * 512:(hf + 1) * 512], q == 0, q == nch - 1,
                           [RW[s_], RaT[q][tt // 8]], [RB[bk]])
                    cs = slice(hf * 512, (hf + 1) * 512)
                    dve(lambda: V.tensor_tensor(out=TD[ti][:, cs], in0=bank(bk), in1=GBC[:, 1, cs], op=ALU.mult),
                        [RB[bk], Rgbc], [Rtd[ti]])
                dve(lambda: V.tensor_tensor(out=x1[:, tt, :], in0=x1[:, tt, :], in1=TD[ti][:], op=ALU.add),
                    [Rtd[ti]], [RX1[tt]])
                if f1 == NFC:
                    S.dma("pool", out_d[b, tt * 128:(tt + 1) * 128, :], x1[:, tt, :], reads=[RX1[tt]])
            for (_s, _w, _k) in sd:
                w_release(_k)
        out_res = RX1

    nc._dbg_names = list(dbg_d)
    S.finish(out_res)
    st.close()
    return nc


_CONST_CACHE = {}


def host_consts():
    if _CONST_CACHE:
        return _CONST_CACHE
    c = {}
    c["identf"] = np.eye(128, dtype=np.float32)
    bd = np.zeros((128, 128), np.float32)
    bd[:64, :64] = 1.0
    bd[64:, 64:] = 1.0
    c["bdones"] = bd
    sel = np.zeros((4, 4, 128), np.float32)
    for b in range(4):
        sel[b, b, :] = 1.0
    c["onesel"] = sel.reshape(4, 512)
    slopes = np.array([2.0 ** (-8.0 * (h + 1) / 4) for h in range(4)], np.float64)
    kk = np.arange(128)
    dg = np.zeros((128, 4, 128), np.float32)
    for h in range(4):
        dg[:, h, :] = -8.0 * slopes[h] * np.abs(kk[None, :] - kk[:, None])
    c["diagb"] = dg.reshape(128, 512)
    pos = np.arange(SEQ)
    hi, lo = (pos // 64) * 64, pos % 64
    kaug = np.zeros((8, 8, SEQ), np.float32)
    qaug = np.zeros((8, 8, SEQ), np.float32)
    for h in range(4):
        m8 = 8.0 * slopes[h]
        for cc in range(2):
            hc = 2 * h + cc
            for r0 in (0, 4):
                kaug[r0 + 0, hc] = 1.0
                kaug[r0 + 1, hc] = 1.0
                kaug[r0 + 2, hc] = m8 * hi
                kaug[r0 + 3, hc] = m8 * lo
            qaug[0, hc] = -m8 * hi
            qaug[1, hc] = -m8 * lo
            qaug[2, hc] = 1.0
            qaug[3, hc] = 1.0
            qaug[4, hc] = 2.0 * m8 * hi
            qaug[5, hc] = 2.0 * m8 * lo
            qaug[6, hc] = -2.0
            qaug[7, hc] = -2.0
    c["kaug"] = kaug.reshape(8, 8 * SEQ)
    qa = qaug.reshape(8, 8, 8, 256)
    c["qaug"] = np.ascontiguousarray(qa.transpose(2, 0, 1, 3)).reshape(8, 8, 8 * 256)
    _CONST_CACHE.update(c)
    return c


def na_table(rpb):
    cq = np.arange(64)
    cs = np.clip(cq - 8, 0, 48)
    ck = np.arange(64)
    colvalid = (ck[:, None] >= cs[None, :]) & (ck[:, None] < cs[None, :] + 16)
    dc = ck[:, None] - cq[None, :] + 15
    dcc = np.clip(dc, 0, 30)
    tab = np.full((8, 2, 64, 12, 2, 64), NEG, np.float32)
    for slot in range(12):
        if slot < 7:
            jp, masked = slot - 1, False
        else:
            jp, masked = slot - 7, True
        for half in range(2):
            for par in range(2):
                dr = 2 * jp + half - 4 - par
                if abs(dr) > 7:
                    continue
                if masked and not (-4 <= dr <= 3):
                    continue
                vals = rpb[:, dr + 7, :][:, dcc]
                tab[:, half, :, slot, par, :] = np.where(colvalid[None], vals, np.float32(NEG))
    return np.ascontiguousarray(tab.reshape(8, 128, 12 * 2 * 64))


_PROG = {}


def kernel(x, c, ada_w, ada_b, norm1_g, w_in, na_q_g, na_k_g, na_rpb, df_q_g, df_k_g,
           lam_q1, lam_k1, lam_q2, lam_k2, df_subln_g, w_na_proj, w_df_proj, w_gate, b_gate,
           w_out, norm2_g, w_up, conv_w, conv_b, w_down, _ncores=8, _nseq=4, _dbg=False):
    f = lambda a: np.ascontiguousarray(np.asarray(a, dtype=np.float32))
    x = f(x); c = f(c)
    hc = host_consts()
    pp = np.zeros((128, PP_COLS), np.float32)
    pp[:, PP_N1G:PP_N1G + 8] = f(norm1_g)[0].reshape(8, 128).T
    pp[:, PP_N2G:PP_N2G + 8] = f(norm2_g)[0].reshape(8, 128).T
    pp[:, PP_BG:PP_BG + 16] = f(b_gate)[0].reshape(16, 128).T
    cw = f(conv_w)[0]
    for i in range(3):
        pp[:, PP_CW + 44 * i:PP_CW + 44 * (i + 1)] = cw[i].reshape(44, 128).T
    pp[:, PP_CB:PP_CB + 44] = f(conv_b)[0].reshape(44, 128).T
    for k, g in enumerate((na_q_g, na_k_g, df_q_g, df_k_g)):
        pp[:, PP_G + k] = np.concatenate([f(g)[0], f(g)[0]])
    lamv = np.concatenate([f(lam_q1)[0], f(lam_q2)[0], f(lam_k1)[0], f(lam_k2)[0]])[None, :]
    lamv = np.ascontiguousarray(np.broadcast_to(lamv, (128, 256)))
    subln = np.ascontiguousarray(np.broadcast_to(f(df_subln_g)[0][None, :], (128, 128)))
    natab = na_table(f(na_rpb)[0])
    shared = {
        "ada_w": f(ada_w)[0], "ada_b": f(ada_b)[0][None, :], "w_in": f(w_in)[0], "w_na_proj": f(w_na_proj)[0],
        "w_df_proj": f(w_df_proj)[0], "w_gate": f(w_gate)[0], "w_out": f(w_out)[0], "w_up": f(w_up)[0],
        "w_down": f(w_down)[0], "pp": pp, "lamv": lamv, "subln": subln, "natab": natab,
    }
    shared.update(hc)
    key = (_nseq, _dbg)
    if key not in _PROG:
        _PROG[key] = build_program(_nseq, _dbg)
    nc = _PROG[key]
    in_maps = []
    for i in range(_ncores):
        xs = x[i * _nseq:(i + 1) * _nseq]
        cs = c[i * _nseq:(i + 1) * _nseq]
        cT = np.ascontiguousarray(cs.reshape(_nseq, 8, 128).transpose(2, 1, 0))
        m = dict(shared)
        m["x"] = xs
        m["cT"] = cT
        in_maps.append(m)
    res = run_bass_kernel_spmd(nc, in_maps, core_ids=list(range(_ncores)))
    outs = [r["out"] for r in res.results]
    return np.concatenate(outs, axis=0).astype(np.float32)
```

```python
import numpy as np
import concourse.bass as bass
import concourse.mybir as mybir

F32 = mybir.dt.float32
BF16 = mybir.dt.bfloat16
AF = mybir.ActivationFunctionType
ALU = mybir.AluOpType
AX = mybir.AxisListType


class Res:
    __slots__ = ("name", "w", "r", "dsem", "dcnt")

    def __init__(self, name):
        self.name = name
        self.w = None
        self.r = {}
        self.dsem = None
        self.dcnt = 0


class Sched:
    def __init__(self, nc, stack):
        self.nc = nc
        self.stack = stack
        self.eng = {"pe": nc.tensor, "act": nc.scalar, "dve": nc.vector,
                    "pool": nc.gpsimd, "sp": nc.sync}
        self.sem = {}
        self.cnt = {}
        self.seen = {}
        for k in self.eng:
            self.sem[k] = stack.enter_context(nc.semaphore("s_" + k))
            self.cnt[k] = 0
            self.seen[k] = {}
        self.nres = 0

    def res(self, name=None):
        self.nres += 1
        return Res(name or f"r{self.nres}")

    def dma_sem(self, r):
        if r.dsem is None:
            self.nres += 1
            r.dsem = self.stack.enter_context(self.nc.semaphore(f"d{self.nres}_" + r.name))
        return r.dsem

    def _wait(self, e, dep):
        kind, who, cnt = dep
        key = who if kind == "e" else ("d", id(who))
        if kind == "e" and who == e and e == "pe":
            return
        if self.seen[e].get(key, 0) >= cnt:
            return
        self.seen[e][key] = cnt
        sem = self.sem[who] if kind == "e" else who.dsem
        self.eng[e].wait_ge(sem, cnt)

    def _deps(self, e, reads, writes):
        for r in reads:
            if r.w is not None:
                self._wait(e, r.w)
        for w in writes:
            if w.w is not None:
                self._wait(e, w.w)
            for k, c in w.r.items():
                if isinstance(k, tuple):
                    self._wait(e, ("d", k[1], c))
                else:
                    self._wait(e, ("e", k, c))

    def op(self, e, fn, reads=(), writes=(), sig=True):
        self._deps(e, reads, writes)
        ins = fn()
        if sig:
            self.cnt[e] += 1
            ins.then_inc(self.sem[e], 1)
            c = self.cnt[e]
        else:
            c = self.cnt[e] + 1
        for r in reads:
            r.r[e] = c
        for w in writes:
            w.w = ("e", e, c)
            w.r = {}
        return ins

    def dma(self, q, out, in_, reads=(), writes=(), **kw):
        self._deps(q, reads, writes)
        carrier = writes[0] if writes else reads[0]
        sem = self.dma_sem(carrier)
        ins = self.eng[q].dma_start(out=out, in_=in_, **kw)
        ins.then_inc(sem, 16)
        carrier.dcnt += 16
        c = carrier.dcnt
        for r in reads:
            r.r[("d", carrier)] = c
        for w in writes:
            w.w = ("d", carrier, c)
            w.r = {}
        return ins

    def barrier_all(self):
        for e in self.eng:
            for o in self.eng:
                if o != e and self.cnt[o] > 0:
                    self._wait(e, ("e", o, self.cnt[o]))

    def finish(self, resources):
        for r in resources:
            if r.w is not None:
                self._wait("sp", r.w)
            for k, c in r.r.items():
                if isinstance(k, tuple):
                    self._wait("sp", ("d", k[1], c))
                else:
                    self._wait("sp", ("e", k, c))

from contextlib import ExitStack
from concourse.bass_utils import run_bass_kernel_spmd

D = 1024
SEQ = 2048
NTT = 16
EPS = 1e-6
NEG = -30000.0
LAM_INIT = 0.8 - 0.6 * 1.0
D_FF = 2816
NFC = 22
ARENA_BASE = 16640
ARENA_END = 229376

PP_N1G, PP_N2G, PP_BG, PP_CW, PP_CB, PP_G = 0, 8, 16, 32, 164, 208
PP_COLS = 212
THIRDS = [(0, 8), (8, 15), (15, 22)]


def na_pairs(qt):
    r = 2 * qt
    if r < 4:
        off = 2 - r // 2
        return [(j, j + off + 1) for j in range(4)]
    if r >= 28:
        off = -((r - 28) // 2)
        return [(12 + j, j + off + 1) for j in range(4)]
    return [((r - 4) // 2 + j, 7 + j) for j in range(5)]


def build_program(nseq, dbg=False):
    nc = bass.Bass("TRN2", target_bir_lowering=False)
    din = lambda n, s: nc.dram_tensor(n, s, F32, kind="ExternalInput").ap()
    x_d = din("x", [nseq, SEQ, D])
    cT_d = din("cT", [128, 8, nseq])
    adaw_d = din("ada_w", [D, 6 * D])
    adab_d = din("ada_b", [1, 6 * D])
    win_d = din("w_in", [D, 3072])
    wna_d = din("w_na_proj", [512, D])
    wdf_d = din("w_df_proj", [512, D])
    wg_d = din("w_gate", [D, 2048])
    wo_d = din("w_out", [D, D])
    wup_d = din("w_up", [D, 2 * D_FF])
    wdn_d = din("w_down", [D_FF, D])
    pp_d = din("pp", [128, PP_COLS])
    lamv_d = din("lamv", [128, 256])
    subln_d = din("subln", [128, 128])
    identf_d = din("identf", [128, 128])
    bdones_d = din("bdones", [128, 128])
    onesel_d = din("onesel", [4, 512])
    diagb_d = din("diagb", [128, 512])
    kaug_d = din("kaug", [8, 8 * SEQ])
    qaug_d = din("qaug", [8, 8, 8 * 256])
    natab_d = din("natab", [8, 128, 1536])
    out_d = nc.dram_tensor("out", [nseq, SEQ, D], F32, kind="ExternalOutput").ap()
    dbg_d = {}

    st = ExitStack()
    S = Sched(nc, st)
    V, A, P, T, G_ = nc.vector, nc.scalar, nc.gpsimd, nc.tensor, None

    cur = [ARENA_BASE]

    def alloc(name, shape, dt, at=None):
        nb = int(np.prod(shape[1:])) * (4 if dt == F32 else 2)
        if at is None:
            off = cur[0]
            cur[0] += (nb + 31) // 32 * 32
            assert cur[0] <= ARENA_END, (name, cur[0])
        else:
            off = at
        return nc.alloc_sbuf_tensor_at(name, list(shape), dt, offset=off)

    PPt = alloc("PPt", [128, PP_COLS], F32)
    IDF = alloc("IDF", [4, 4], F32)
    IDB = alloc("IDB", [128, 128], BF16)
    BD = alloc("BD", [128, 128], BF16)
    ONESEL = alloc("ONESEL", [4, 512], F32)
    DIAGB = alloc("DIAGB", [128, 4, 128], BF16)
    MODT = alloc("MODT", [128, 48, nseq], F32)
    A1 = alloc("A1", [128, 8, nseq], F32)
    A2 = alloc("A2", [128, 8, nseq], F32)
    MODG = alloc("MODG", [4, 2048], F32)
    GBC = alloc("GBC", [128, 2, 1024], F32)
    LAMC = alloc("LAMC", [128, 8], F32)
    SUBLN = alloc("SUBLN", [128, 128], F32)
    STAT = alloc("STAT", [128, 64], F32)
    ONES1 = alloc("ONES1", [1, 4], F32)
    CONS = alloc("CONS", [128, 2], F32)
    WS = [alloc(f"WS{i}", [128, 4096], BF16) for i in range(3)]
    RH = cur[0]; cur[0] += 32768
    RY = cur[0]; cur[0] += 32768
    RM = cur[0]; cur[0] += 32768
    RX = cur[0]; cur[0] += 65536
    assert cur[0] <= ARENA_END, cur[0]

    hT = alloc("hT", [128, 8, SEQ], BF16, at=RH)
    ynaT = alloc("ynaT", [128, 4, SEQ], BF16, at=RY)
    ydfT = alloc("ydfT", [128, 4, SEQ], BF16, at=RY + 16384)
    aT = alloc("aT", [128, 8, SEQ], BF16, at=RY)
    mT = alloc("mT", [128, 8, SEQ], BF16, at=RM)
    x1 = alloc("x1", [128, NTT, D], F32, at=RX)
    XT = [alloc(f"XT{i}", [128, 4, D], F32, at=RX + i * 16384) for i in range(2)]
    XN = [alloc(f"XN{i}", [128, 4, D], BF16, at=RX + 32768 + i * 8192) for i in range(2)]
    JUNK = alloc("JUNK", [128, D], BF16, at=RX + 49152)
    AW = [alloc(f"AW{i}", [128, 8, 512], F32, at=RX + i * 16384) for i in range(2)]
    MODSB = alloc("MODSB", [4, 6 * D], F32, at=RX + 32768)
    ADAB = alloc("ADAB", [1, 6 * D], F32, at=RM)
    CTs = alloc("CTs", [128, 8, nseq], F32, at=RM + 24576)
    SCs = alloc("SCs", [128, 8, nseq], F32, at=RM + 24576 + 256)
    LAMV = alloc("LAMV", [128, 256], F32, at=RM + 24576 + 512)
    LT = alloc("LT", [128, 128], F32, at=RM + 24576 + 2048)
    KN = alloc("KN", [128, 4, SEQ], BF16, at=RX)
    QN = alloc("QN", [128, 4, SEQ], BF16, at=RX + 16384)
    VN = alloc("VN", [128, NTT, 8, 65], BF16, at=RX + 32768)
    TAB = [alloc(f"TAB{i}", [128, 12, 2, 64], F32, at=RX + 49408 + i * 6144) for i in range(2)]
    EN = [alloc(f"EN{i}", [128, 640], BF16, at=RX + 61696 + i * 1280) for i in range(3)]
    ENP = [alloc(f"ENP{i}", [128, 1280], BF16, at=RY + 16384 + i * 2560) for i in range(3)]
    TAB2 = alloc("TAB2", [128, 12, 2, 64], F32, at=RM + 26624)
    TAB3 = alloc("TAB3", [128, 12, 2, 64], F32, at=RY + 16384 + 7680)
    NTAB = [TAB[0], TAB[1], TAB2, TAB3]
    SQ = [alloc(f"SQ{i}", [128, 512], BF16, at=RM + i * 1024) for i in range(2)]
    LNT = [alloc(f"LNT{i}", [128, 512], F32, at=RM + 2048 + i * 2048) for i in range(2)]
    RS = [alloc(f"RS{i}", [128, 512], F32, at=RM + 6144 + i * 2048) for i in range(2)]
    YNATOK = alloc("YNATOK", [128, NTT, 512], BF16, at=RM + 10240)
    RC = alloc("RC", [128, 16], F32, at=RX + 61696)
    KB = alloc("KB", [128, 8, SEQ], BF16, at=RX)
    VD = alloc("VD", [128, NTT, 4, 129], BF16, at=RX + 32768)
    ED = [alloc(f"ED{i}", [128, 4, 256], BF16, at=RX + 49280 + i * 2048) for i in range(3)]
    QB = [alloc(f"QB{i}", [128, 8, 256], BF16, at=RM + 10240 + i * 4096) for i in range(2)]
    O0 = alloc("O0", [128, 2, 128], F32, at=RM + 26688)
    YD = alloc("YD", [128, 2, 4, 128], F32, at=RM + 27712)
    YB = alloc("YB", [128, 2, 512], BF16, at=RX + 55424)
    YSQ = alloc("YSQ", [128, 2, 4, 128], F32, at=RX + 57472)
    GA = [alloc(f"GA{i}", [128, 512], F32, at=RX + i * 2048) for i in range(2)]
    GB = [alloc(f"GB{i}", [128, 512], F32, at=RX + 4096 + i * 2048) for i in range(2)]
    T1 = [alloc(f"T1{i}", [128, 512], F32, at=RX + 8192 + i * 2048) for i in range(2)]
    T2 = [alloc(f"T2{i}", [128, 512], F32, at=RX + 12288 + i * 2048) for i in range(2)]
    XG = [alloc(f"XG{i}", [128, D], F32, at=RY + i * 4096) for i in range(2)]
    TG = [alloc(f"TG{i}", [128, D], F32, at=RY + 8192 + i * 4096) for i in range(2)]
    XNG = alloc("XNG", [128, 4, D], BF16, at=RY + 16384)
    JUNKG = alloc("JUNKG", [128, D], BF16, at=RY + 24576)
    ACC = [alloc(f"ACC{i}", [128, SEQ], F32, at=RM + i * 8192) for i in range(2)]
    GG = alloc("GG", [128, SEQ], F32, at=RM + 16384)
    TD = [alloc(f"TD{i}", [128, D], F32, at=RM + 24576 + i * 4096) for i in range(2)]

    PS = st.enter_context(nc.psum_tensor("PS", [128, 4096], F32))
    bank = lambda i: PS[:, 512 * i:512 * (i + 1)]
    RB = [S.res(f"bank{i}") for i in range(8)]
    RB6b = RB[6]
    RW = [S.res(f"ws{i}") for i in range(3)]
    _rc = {}

    def R(n):
        if n not in _rc:
            _rc[n] = S.res(n)
        return _rc[n]

    def act(out, in_, func, reads, writes, **kw):
        return S.op("act", lambda: A.activation(out=out, in_=in_, func=func, **kw), reads, writes)

    def mm(out, lhsT, rhs, start, stop, reads, writes, sig=None):
        return S.op("pe", lambda: T.matmul(out, lhsT=lhsT, rhs=rhs, start=start, stop=stop),
                    reads, writes, sig=(stop if sig is None else sig))

    def tr(out, in_, reads, writes, sig=True):
        return S.op("pe", lambda: T.transpose(out=out, in_=in_, identity=IDB[:]), reads, writes, sig=sig)

    def dve(fn, reads, writes):
        return S.op("dve", fn, reads, writes)

    def pool(fn, reads, writes):
        return S.op("pool", fn, reads, writes)


    def dump(name, t, shape):
        dd = nc.dram_tensor("dbg_" + name, list(shape), F32, kind="ExternalOutput").ap()
        Rd = R("dbg_" + name)
        S.barrier_all()
        S.dma("pool", dd, t[:], writes=[Rd])
        for e in ("sp", "act", "dve", "pool", "pe"):
            S._deps(e, [Rd], [])
        dbg_d[name] = shape

    NSLOT = 3
    plan = []
    for _b in range(nseq):
        plan += [("win", 512), ("win", 1024), ("win", 0), ("win", 2048), ("win", 2560), ("win", 1536)]
        plan += [("gate", 0), ("proj2", 0), ("gate", 1), ("gate", 2), ("proj2", 1), ("gate", 3)]
        plan += [("win_o", 0), ("win_o", 512)]
        for (_f0, _f1) in THIRDS:
            _fc = _f0
            while _fc < _f1:
                _np = min(2, _f1 - _fc)
                plan.append(("wup", _fc, _np))
                _fc += _np
            for _q in range(0, _f1 - _f0, 4):
                plan.append(("wdn", _f0 + _q, min(4, _f1 - _f0 - _q)))
    wstate = {"issued": 0, "use": 0}
    rr = lambda ap: ap.rearrange("(c p) n -> p c n", p=128)

    def w_issue(k):
        d = plan[k]
        i = k % NSLOT
        rw = [RW[i]]
        if d[0] == "win":
            v = WS[i][:, 0:4096].rearrange("p (c n) -> p c n", c=8)
            S.dma("pool", v, rr(win_d[:, d[1]:d[1] + 512]), writes=rw)
        elif d[0] == "win_o":
            v = WS[i][:, 0:4096].rearrange("p (c n) -> p c n", c=8)
            S.dma("pool", v, rr(wo_d[:, d[1]:d[1] + 512]), writes=rw)
        elif d[0] == "gate":
            ep = d[1]
            v = WS[i][:, 0:4096].rearrange("p (c n) -> p c n", c=8)
            S.dma("pool", v[:, :, 0:256], rr(wg_d[:, ep * 256:(ep + 1) * 256]), writes=rw)
            S.dma("pool", v[:, :, 256:512], rr(wg_d[:, 1024 + ep * 256:1024 + (ep + 1) * 256]), writes=rw)
        elif d[0] == "proj2":
            epp = d[1]
            v = WS[i][:, 0:4096].rearrange("p (a c n) -> p a c n", a=2, c=4)
            S.dma("pool", v[:, 0, :, :], rr(wna_d[:, epp * 512:(epp + 1) * 512]), writes=rw)
            S.dma("pool", v[:, 1, :, :], rr(wdf_d[:, epp * 512:(epp + 1) * 512]), writes=rw)
        elif d[0] == "wup":
            fc, npair = d[1], d[2]
            v = WS[i][:, 0:4096].rearrange("p (c n) -> p c n", c=8)
            S.dma("pool", v[:, :, 0:128 * npair], rr(wup_d[:, fc * 128:(fc + npair) * 128]), writes=rw)
            S.dma("pool", v[:, :, 256:256 + 128 * npair], rr(wup_d[:, D_FF + fc * 128:D_FF + (fc + npair) * 128]), writes=rw)
        elif d[0] == "wdn":
            f, m = d[1], d[2]
            v = WS[i][:, 0:4096].rearrange("p (c n) -> p c n", c=4)
            S.dma("pool", v[:, 0:m, :], rr(wdn_d[f * 128:(f + m) * 128, :]), writes=rw)

    def w_acquire(kind):
        k = wstate["use"]
        wstate["use"] += 1
        assert plan[k][0] == kind, (k, plan[k], kind)
        while wstate["issued"] <= k:
            w_issue(wstate["issued"])
            wstate["issued"] += 1
        i = k % NSLOT
        if kind == "proj2":
            v = WS[i][:, 0:4096].rearrange("p (a c n) -> p a c n", a=2, c=4)
        elif kind == "wdn":
            v = WS[i][:, 0:4096].rearrange("p (c n) -> p c n", c=4)
        else:
            v = WS[i][:, 0:4096].rearrange("p (c n) -> p c n", c=8)
        return k, i, v

    def w_release(k):
        nk = k + NSLOT
        if nk < len(plan) and wstate["issued"] == nk:
            w_issue(nk)
            wstate["issued"] += 1

    Rpp, Ridf, Ridb, Rbd, Rsel, Rdiag, Rmodt, Ra12, Rmodg, Rgbc, Rlam, Rsub, Rstat = [
        R(n) for n in "pp idf idb bd sel diag modt a12 modg gbc lam sub stat".split()]
    Rct, Rsc, Rmodsb, Radab, Rlamv, Rlt, Rones1 = [R(n) for n in "ct sc modsb adab lamv lt ones1".split()]
    Raw = [R("aw0"), R("aw1")]
    S.dma("sp", PPt[:], pp_d, writes=[Rpp])
    S.dma("sp", IDF[:], identf_d[0:4, 0:4], writes=[Ridf])
    S.dma("sp", ONESEL[:], onesel_d, writes=[Rsel])
    S.dma("sp", LAMV[:], lamv_d, writes=[Rlamv])
    S.dma("sp", SUBLN[:], subln_d, writes=[Rsub])
    S.dma("sp", CTs[:], cT_d, writes=[Rct])
    S.dma("sp", ADAB[:], adab_d, writes=[Radab])
    S.dma("pool", IDB[:], identf_d, writes=[Ridb])
    S.dma("pool", BD[:], bdones_d, writes=[Rbd])
    S.dma("pool", DIAGB[:], diagb_d.rearrange("p (h n) -> p h n", h=4), writes=[Rdiag])
    dve(lambda: V.memset(ONES1[:], 1.0), [], [Rones1])
    Rcons = R("cons")
    dve(lambda: V.memset(CONS[:, 0:1], EPS), [], [Rcons])
    S.barrier_all()
    act(SCs[:], CTs[:], AF.Silu, [Rct], [Rsc])
    for ec in range(12):
        S.dma("sp", AW[ec % 2][:], adaw_d[:, ec * 512:(ec + 1) * 512].rearrange("(c p) n -> p c n", p=128),
              writes=[Raw[ec % 2]])
        b = ec % 2
        for dc in range(8):
            mm(bank(b)[0:nseq, :], SCs[:, dc, :], AW[ec % 2][:, dc, :], dc == 0, False,
               [Rsc, Raw[ec % 2]], [RB[b]], sig=False)
        mm(bank(b)[0:nseq, :], ONES1[0:1, 0:nseq], ADAB[0:1, ec * 512:(ec + 1) * 512], False, True,
           [Rones1, Radab], [RB[b]])
        dve(lambda: V.tensor_copy(out=MODSB[0:nseq, ec * 512:(ec + 1) * 512], in_=bank(b)[0:nseq, :]),
            [RB[b]], [Rmodsb])
    for j in range(48):
        mm(bank(2)[:, j * nseq:(j + 1) * nseq], MODSB[0:nseq, j * 128:(j + 1) * 128], IDF[0:nseq, 0:nseq],
           True, True, [Rmodsb, Ridf], [RB[2]], sig=(j == 47))
    dve(lambda: V.tensor_copy(out=MODT[:].rearrange("p a b -> p (a b)"), in_=bank(2)[:, 0:48 * nseq]),
        [RB[2]], [Rmodt])
    for (Ax, sc0, gcol) in ((A1, 8, PP_N1G), (A2, 32, PP_N2G)):
        dve(lambda: V.tensor_scalar(out=Ax[:], in0=MODT[:, sc0:sc0 + 8, :], scalar1=1.0, scalar2=None, op0=ALU.add),
            [Rmodt], [Ra12])
        dve(lambda: V.tensor_tensor(out=Ax[:], in0=Ax[:],
                                    in1=PPt[:, gcol:gcol + 8].unsqueeze(2).to_broadcast([128, 8, nseq]), op=ALU.mult),
            [Ra12, Rpp], [Ra12])
    dve(lambda: V.tensor_copy(out=MODG[0:nseq, 0:1024], in_=MODSB[0:nseq, 2048:3072]), [Rmodsb], [Rmodg])
    dve(lambda: V.tensor_copy(out=MODG[0:nseq, 1024:2048], in_=MODSB[0:nseq, 5120:6144]), [Rmodsb], [Rmodg])
    dve(lambda: V.tensor_tensor(out=LT[:], in0=LAMV[:, 0:128], in1=LAMV[:, 128:256], op=ALU.mult), [Rlamv], [Rlt])
    dve(lambda: V.tensor_reduce(out=LAMC[:, 0:2], in_=LT[:].rearrange("p (a b) -> p a b", a=2), axis=AX.X, op=ALU.add),
        [Rlt], [Rlam])
    act(LAMC[:, 2:4], LAMC[:, 0:2], AF.Exp, [Rlam], [Rlam])
    dve(lambda: V.tensor_tensor(out=LAMC[:, 4:5], in0=LAMC[:, 2:3], in1=LAMC[:, 3:4], op=ALU.subtract), [Rlam], [Rlam])
    dve(lambda: V.tensor_scalar(out=LAMC[:, 5:6], in0=LAMC[:, 4:5], scalar1=-1.0, scalar2=-LAM_INIT,
                                op0=ALU.mult, op1=ALU.add), [Rlam], [Rlam])
    dve(lambda: V.tensor_scalar(out=SUBLN[:], in0=SUBLN[:], scalar1=1.0 - LAM_INIT, scalar2=None, op0=ALU.mult),
        [Rsub], [Rsub])
    S.barrier_all()

    for _k in range(NSLOT):
        w_issue(_k)
    wstate["issued"] = NSLOT

    gcolAP = lambda k, p0, p1: PPt[p0:p1, PP_G + k:PP_G + k + 1]
    rot = {"sq": 0, "bk": 0}

    def qknorm_a(src_ps, Rsrc, Pn, N):
        i = rot["sq"] % 2
        rot["sq"] += 1
        Rsq, Rln, Rrs = qk_res[i]
        act(SQ[i][0:Pn, 0:N], src_ps, AF.Square, [Rsrc], [Rsq])
        return i

    def qknorm_b(i, src_ps, Rsrc, Pn, N, gk, dst_list, Rdst, ssbank, Rss):
        Rsq, Rln, Rrs = qk_res[i]
        mm(ssbank[0:Pn, 0:N], BD[0:Pn, 0:Pn], SQ[i][0:Pn, 0:N], True, True, [Rbd, Rsq], [Rss])
        act(LNT[i][0:Pn, 0:N], ssbank[0:Pn, 0:N], AF.Ln, [Rss, Rcons], [Rln], scale=1.0 / 64, bias=CONS[0:Pn, 0:1])
        act(RS[i][0:Pn, 0:N], LNT[i][0:Pn, 0:N], AF.Exp, [Rln], [Rrs], scale=-0.5)
        for (dst, p0, p1) in dst_list:
            dve(lambda: V.scalar_tensor_tensor(out=dst, in0=src_ps[p0:p1, :], scalar=gcolAP(gk, p0, p1),
                                               in1=RS[i][p0:p1, 0:N], op0=ALU.mult, op1=ALU.mult),
                [Rsrc, Rrs, Rpp], [Rdst])

    def qknorm(src_ps, Rsrc, Pn, N, gk, dst_list, Rdst, ssbank, Rss):
        i = qknorm_a(src_ps, Rsrc, Pn, N)
        qknorm_b(i, src_ps, Rsrc, Pn, N, gk, dst_list, Rdst, ssbank, Rss)

    qk_res = [(R("sq0"), R("ln0"), R("rs0")), (R("sq1"), R("ln1"), R("rs1"))]

    def norm_to_T(b, srcs, Rsrcs, Acoef, Bcol0, dstT, RdstT, XNbuf, Rxn, junk, Rjunk, tq, pbanks, al_xn=(), al_junk=(), part=0):
        t0_ = 4 * tq
        for i in range(4 if part != 2 else 0):
            tt = t0_ + i
            act(junk[:], srcs[i], AF.Square, [Rsrcs[i]], [Rjunk, Rstat] + list(al_junk), accum_out=STAT[:, tt:tt + 1])
        if part != 2:
            act(STAT[:, 16 + t0_:20 + t0_], STAT[:, t0_:t0_ + 4], AF.Ln, [Rstat, Rcons], [Rstat], scale=1.0 / D, bias=CONS[:, 0:1])
            act(STAT[:, 32 + t0_:36 + t0_], STAT[:, 16 + t0_:20 + t0_], AF.Exp, [Rstat], [Rstat], scale=-0.5)
        for i in range(4 if part != 2 else 0):
            tt = t0_ + i
            dve(lambda: V.tensor_scalar(out=XNbuf[:, i, :], in0=srcs[i], scalar1=STAT[:, 32 + tt:33 + tt], scalar2=None,
                                        op0=ALU.mult), [Rsrcs[i], Rstat], [Rxn] + list(al_xn))
        for c in range(8 if part != 1 else 0):
            pb = pbanks[c % len(pbanks)]
            ptv = bank(pb).bitcast(BF16)
            for i in range(4):
                tr(ptv[:, i * 128:(i + 1) * 128], XNbuf[:, i, c * 128:(c + 1) * 128], [Rxn, Ridb], [RB[pb]], sig=(i == 3))
            dst = dstT[:, c, tq * 512:(tq + 1) * 512]
            sc = Acoef[:, c, b:b + 1]
            bi = MODT[:, Bcol0 + c, b:b + 1]
            if c % 2 == 0:
                act(dst, ptv[:, 0:512], AF.Identity, [RB[pb], Ra12, Rmodt], [RdstT[tq][c]], scale=sc, bias=bi)
            else:
                dve(lambda: V.tensor_scalar(out=dst, in0=ptv[:, 0:512], scalar1=sc, scalar2=bi, op0=ALU.mult, op1=ALU.add),
                    [RB[pb], Ra12, Rmodt], [RdstT[tq][c]])

    out_res = []
    RX1g = [R(f"x1_{t}") for t in range(NTT)]
    for b in range(nseq):
        RhT = [[R(f"hT{t}_{c}") for c in range(8)] for t in range(4)]
        RhT_all = [r for l in RhT for r in l]
        for gi in range(2):
            for hf in range(2):
                bk = 6 + hf
                mm(bank(bk), ONESEL[0:nseq, b * 128:(b + 1) * 128], MODG[0:nseq, gi * 1024 + hf * 512: gi * 1024 + (hf + 1) * 512],
                   True, True, [Rsel, Rmodg], [RB[bk]])
                dve(lambda: V.tensor_copy(out=GBC[:, gi, hf * 512:(hf + 1) * 512], in_=bank(bk)), [RB[bk]], [Rgbc])
        Rxt = [[R(f"xt{i}_{k}") for k in range(4)] for i in range(2)]
        Rxn = [R("xn0"), R("xn1")]
        Rjunk = R("junk")
        def phA(tq, part):
            bi = tq % 2
            if part == 1:
                for i in range(4):
                    tt = 4 * tq + i
                    S.dma("sp", XT[bi][:, i, :], x_d[b, tt * 128:(tt + 1) * 128, :], writes=[Rxt[bi][i], RX1g[4 * bi + i]])
            norm_to_T(b, [XT[bi][:, i, :] for i in range(4)], Rxt[bi], A1, 0, hT, RhT, XN[bi], Rxn[bi], JUNK, Rjunk, tq, [0, 1],
                      al_xn=[RX1g[8 + 2 * bi], RX1g[9 + 2 * bi]], al_junk=RX1g[12:16], part=part)

        phA(0, 1)
        phA(1, 1)
        phA(0, 2)
        phA(2, 1)
        phA(1, 2)
        phA(3, 1)
        phA(2, 2)
        phA(3, 2)
        S.barrier_all()
        if dbg and b == 0:
            dump("hT", hT, [128, 8, SEQ])

        RKN = [[R(f"kn{c}_{t}") for t in range(4)] for c in range(4)]
        RQN = [[R(f"qn{c}_{t}") for t in range(4)] for c in range(4)]
        RVN = [R(f"vn{t}") for t in range(NTT)]
        Rvn1 = R("vnones")
        dve(lambda: V.memset(VN[:, :, :, 64:65], 1.0), [], [Rvn1])
        for (dstT_, Rd, c0, gk) in ((KN, RKN, 512, 1), (None, None, 1024, None), (QN, RQN, 0, 0)):
            wk, si, Wv = w_acquire("win")
            if dstT_ is None:
                for tt in range(NTT):
                    bk = 4 + tt % 2
                    for dc in range(8):
                        mm(bank(bk), hT[:, dc, tt * 128:(tt + 1) * 128], Wv[:, dc, :], dc == 0, dc == 7,
                           [RW[si]] + ([r for l in RhT for r in l] if dc == 0 else []), [RB[bk]])
                    dstv = VN[:, tt, :, 0:64]
                    srcv = bank(bk).rearrange("p (h d) -> p h d", h=8)
                    if tt % 2 == 0:
                        act(dstv, srcv, AF.Copy, [RB[bk]], [RVN[tt]])
                    else:
                        dve(lambda: V.tensor_copy(out=dstv, in_=srcv), [RB[bk]], [RVN[tt]])
            else:
                for ch in range(4):
                    for tq in range(4):
                        bk = rot["bk"] % 2
                        rot["bk"] += 1
                        for dc in range(8):
                            mm(bank(bk), Wv[:, dc, ch * 128:(ch + 1) * 128], hT[:, dc, tq * 512:(tq + 1) * 512], dc == 0, dc == 7,
                               [RW[si]] + (RhT[tq] if dc == 0 else []), [RB[bk]])
                        dst = dstT_[:, ch, tq * 512:(tq + 1) * 512]
                        qknorm(bank(bk), RB[bk], 128, 512, gk, [(dst, 0, 128)], Rd[ch][tq], bank(2 + bk), RB[2 + bk])
            w_release(wk)
        if dbg and b == 0:
            dump("KN", KN, [128, 4, SEQ])
            dump("QN", QN, [128, 4, SEQ])
            dump("VN", VN, [128, NTT, 8, 65])

        Rtab = [R(f"tab{i}") for i in range(4)]
        Ren = [R(f"en{i}") for i in range(3)]
        Rytok = [R(f"ytok{t}") for t in range(NTT)]
        Rrc = R("rc")
        RYna = [R(f"yna{t}") for t in range(NTT)]
        RKN_all = [r for l in RKN for r in l]
        RQN_all = [r for l in RQN for r in l]
        Rrcs = [R("rcA"), R("rcB")]

        def na_front(it, ch, qt):
            pairs = na_pairs(qt)
            n = len(pairs)
            sb = it % 2
            eb = it % 3
            base = 1536 * sb
            Rs_ = [RB[3 * sb], RB[3 * sb + 1], RB[3 * sb + 2]]
            for j, (vt, slot) in enumerate(pairs):
                for hh in range(2):
                    p0, p1 = 64 * hh, 64 * hh + 64
                    o = base + 640 * hh + j * 128
                    mm(PS[:, o:o + 128], KN[p0:p1, ch, vt * 128:(vt + 1) * 128],
                       QN[p0:p1, ch, qt * 128:(qt + 1) * 128], True, True,
                       (RKN_all + RQN_all) if (j == 0 and hh == 0 and it == 0) else [], Rs_, sig=(j == n - 1 and hh == 1))
            s0 = pairs[0][1]
            for hh in range(2):
                tb = 2 * (ch % 2) + hh
                Sv4 = PS[:, base + 640 * hh:base + 640 * hh + n * 128].rearrange("p (j a c) -> p j a c", j=n, a=2)
                dve(lambda: V.scalar_tensor_tensor(out=Sv4, in0=Sv4, scalar=0.125, in1=NTAB[tb][:, s0:s0 + n, :, :],
                                                   op0=ALU.mult, op1=ALU.add), Rs_ + [Rtab[tb]], Rs_)
            w = 640 + n * 128
            act(ENP[eb][:, 0:w], PS[:, base:base + w], AF.Exp, Rs_, [Ren[eb]])

        def na_back(it, ch, qt):
            pairs = na_pairs(qt)
            n = len(pairs)
            eb = it % 3
            pvb = 6 + it % 2
            ri = it % 2
            for hh in range(2):
                h = 2 * ch + hh
                for j, (vt, slot) in enumerate(pairs):
                    S.op("pe", lambda: T.matmul(bank(pvb)[:, 65 * hh:65 * hh + 65], lhsT=ENP[eb][:, 640 * hh + j * 128:640 * hh + (j + 1) * 128],
                                                rhs=VN[:, vt, h, :], start=(j == 0 and hh == 0), stop=(j == n - 1 and hh == 1),
                                                skip_group_check=True),
                         [Ren[eb], Rvn1] + (RVN if (j == 0 and hh == 0 and it == 0) else []), [RB[pvb]],
                         sig=(j == n - 1 and hh == 1))
            pov = bank(pvb)[:, 0:130].rearrange("p (a n) -> p a n", a=2)
            dve(lambda: V.reciprocal(out=RC[:, 2 * ri:2 * ri + 2], in_=pov[:, :, 64]), [RB[pvb]], [Rrcs[ri]])
            for hh in range(2):
                h = 2 * ch + hh
                act(YNATOK[:, qt, h * 64:(h + 1) * 64], pov[:, hh, 0:64], AF.Copy, [RB[pvb], Rrcs[ri]], [Rytok[qt]],
                    scale=RC[:, 2 * ri + hh:2 * ri + hh + 1])

        items = [(ch, qt) for ch in range(4) for qt in range(NTT)]

        def load_tabs(ch):
            for hh in range(2):
                tb = 2 * (ch % 2) + hh
                S.dma("sp", NTAB[tb][:].rearrange("p a b c -> p (a b c)"), natab_d[2 * ch + hh], writes=[Rtab[tb]])

        load_tabs(0)
        for it, (ch, qt) in enumerate(items):
            if qt == 0 and ch + 1 < 4:
                load_tabs(ch + 1)
            na_front(it, ch, qt)
            if it >= 2:
                na_back(it - 2, *items[it - 2])
        na_back(len(items) - 2, *items[-2])
        na_back(len(items) - 1, *items[-1])
        for tt in range(NTT):
            pb = 6 + tt % 2
            ptv = bank(pb).bitcast(BF16)
            for cc in range(4):
                tr(ptv[:, cc * 128:(cc + 1) * 128], YNATOK[:, tt, cc * 128:(cc + 1) * 128], [Rytok[tt], Ridb], [RB[pb]], sig=(cc == 3))
            dve(lambda: V.tensor_copy(out=ynaT[:, :, tt * 128:(tt + 1) * 128], in_=ptv[:, 0:512].rearrange("p (c t) -> p c t", c=4)),
                [RB[pb]], [RYna[tt]])
        S.barrier_all()
        if dbg and b == 0:
            dump("ynaT", ynaT, [128, 4, SEQ])

        RKB = [R(f"kb{i}") for i in range(4)]
        Rkaug = R("kaug")
        RVD = [R(f"vd{t}") for t in range(NTT)]
        Rvd1 = R("vdones")
        S.dma("pool", KB[64:72, :, :], kaug_d.rearrange("r (a n) -> r a n", a=8), writes=[Rkaug])
        dve(lambda: V.memset(VD[:, :, :, 128:129], 1.0), [], [Rvd1])
        wk, si, Wv = w_acquire("win")
        for hp in range(4):
            for tq in range(4):
                bk = rot["bk"] % 2
                rot["bk"] += 1
                for dc in range(8):
                    mm(bank(bk), Wv[:, dc, hp * 128:(hp + 1) * 128], hT[:, dc, tq * 512:(tq + 1) * 512], dc == 0, dc == 7,
                       [RW[si]], [RB[bk]])
                tok = slice(tq * 512, (tq + 1) * 512)
                qknorm(bank(bk), RB[bk], 128, 512, 3,
                       [(KB[0:64, 2 * hp, tok], 0, 64), (KB[0:64, 2 * hp + 1, tok], 64, 128)], RKB[hp],
                       bank(2 + bk), RB[2 + bk])
        w_release(wk)
        wk, si, Wv = w_acquire("win")
        for tt in range(NTT):
            bk = 4 + tt % 2
            for dc in range(8):
                mm(bank(bk), hT[:, dc, tt * 128:(tt + 1) * 128], Wv[:, dc, :], dc == 0, dc == 7, [RW[si]], [RB[bk]])
            dstv = VD[:, tt, :, 0:128]
            srcv = bank(bk).rearrange("p (h d) -> p h d", h=4)
            if tt % 2 == 0:
                act(dstv, srcv, AF.Copy, [RB[bk]], [RVD[tt]])
            else:
                dve(lambda: V.tensor_copy(out=dstv, in_=srcv), [RB[bk]], [RVD[tt]])
        w_release(wk)
        wkq, siq, WQ = w_acquire("win")
        if dbg and b == 0:
            dump("KB", KB, [128, 8, SEQ])
            dump("VD", VD, [128, NTT, 4, 129])

        RQB = [[R(f"qb{s}_{i}") for i in range(4)] for s in range(2)]
        RQaug = [R("qaug0"), R("qaug1")]
        Red = [R(f"ed{i}") for i in range(3)]
        Ro0, Ryd, Ryb, Rysq = R("o0"), R("yd"), R("yb"), R("ysq")
        Rrct = R("rct")
        RYdf = [R(f"ydf{t}") for t in range(NTT)]

        def q_aug(jt):
            qs = jt % 2
            S.dma("pool", QB[qs][64:72, :, :], qaug_d[jt].rearrange("r (a n) -> r a n", a=8), writes=[RQaug[qs]])

        qst = {}

        def q_proj_a(jt, hp, bk=6):
            for dc in range(8):
                mm(bank(bk)[:, 0:256], WQ[:, dc, hp * 128:(hp + 1) * 128], hT[:, dc, jt * 256:(jt + 1) * 256], dc == 0, dc == 7,
                   [RW[siq]], [RB[bk]])
            qst[(jt, hp)] = qknorm_a(bank(bk)[:, 0:256], RB[bk], 128, 256)

        def q_proj_b(jt, hp, bk=6):
            qs = jt % 2
            qknorm_b(qst.pop((jt, hp)), bank(bk)[:, 0:256], RB[bk], 128, 256, 2,
                     [(QB[qs][0:64, 2 * hp, :], 0, 64), (QB[qs][0:64, 2 * hp + 1, :], 64, 128)], RQB[qs][hp],
                     bank(bk)[:, 256:512], RB[bk])

        def d_front(gi, jt, h, c, g):
            qs = jt % 2
            hc = 2 * h + c
            sb = gi % 2
            eb = gi % 3
            Rs_ = [RB[2 * sb], RB[2 * sb + 1]]
            base = 1024 * sb
            rq = [RQB[qs][h], RQaug[qs], RKB[h], Rkaug]
            for kk in range(4):
                kt = 4 * g + kk
                o = base + kk * 256
                last = (kk == 3)
                kslice = slice(kt * 128, (kt + 1) * 128)
                if kt < 2 * jt:
                    mm(PS[:, o:o + 256], KB[0:68, hc, kslice], QB[qs][0:68, hc, :], True, True, rq, Rs_, sig=last)
                elif kt > 2 * jt + 1:
                    mm(PS[:, o:o + 256], KB[0:72, hc, kslice], QB[qs][0:72, hc, :], True, True, rq, Rs_, sig=last)
                else:
                    e = kt - 2 * jt
                    oth = 1 - e
                    kr = 68 if e == 0 else 72
                    mm(PS[:, o + oth * 128:o + (oth + 1) * 128], KB[0:kr, hc, kslice],
                       QB[qs][0:kr, hc, oth * 128:(oth + 1) * 128], True, True, rq, Rs_, sig=False)
                    mm(PS[:, o + e * 128:o + (e + 1) * 128], KB[0:64, hc, kslice],
                       QB[qs][0:64, hc, e * 128:(e + 1) * 128], True, False, rq, Rs_, sig=False)
                    mm(PS[:, o + e * 128:o + (e + 1) * 128], IDB[:, :], DIAGB[:, h, :], False, True,
                       [Ridb, Rdiag], Rs_, sig=last)
            act(ED[eb][:].rearrange("p a b -> p (a b)"), PS[:, base:base + 1024], AF.Exp, Rs_, [Red[eb]], scale=0.125)

        def d_back(gi, jt, h, c, g):
            eb = gi % 3
            accb = 4 + c
            for kk in range(4):
                kt = 4 * g + kk
                for s_ in range(2):
                    S.op("pe", lambda: T.matmul(bank(accb)[:, s_ * 129:(s_ + 1) * 129], lhsT=ED[eb][:, kk, s_ * 128:(s_ + 1) * 128],
                                                rhs=VD[:, kt, h, :], start=(kt == 0 and s_ == 0), stop=(kt == 15 and s_ == 1),
                                                skip_group_check=True),
                         [Red[eb], RVD[kt], Rvd1], [RB[accb]], sig=(kk == 3 and s_ == 1))
            if g != 3:
                return
            accv = bank(accb)[:, 0:258].rearrange("p (s n) -> p s n", s=2)
            if c == 0:
                dve(lambda: V.reciprocal(out=RC[:, 2:4], in_=accv[:, :, 128]), [RB[accb]], [Rrc])
                for s_ in range(2):
                    dve(lambda: V.tensor_scalar(out=O0[:, s_, :], in0=accv[:, s_, 0:128], scalar1=RC[:, 2 + s_:3 + s_],
                                                scalar2=None, op0=ALU.mult), [RB[accb], Rrc], [Ro0])
            else:
                dve(lambda: V.reciprocal(out=RC[:, 4:6], in_=accv[:, :, 128]), [RB[accb]], [Rrc])
                dve(lambda: V.tensor_scalar(out=RC[:, 6:8], in0=RC[:, 4:6], scalar1=LAMC[:, 5:6], scalar2=None,
                                            op0=ALU.mult), [Rrc, Rlam], [Rrc])
                for s_ in range(2):
                    dve(lambda: V.scalar_tensor_tensor(out=YD[:, s_, h, :], in0=accv[:, s_, 0:128], scalar=RC[:, 6 + s_:7 + s_],
                                                       in1=O0[:, s_, :], op0=ALU.mult, op1=ALU.add),
                        [RB[accb], Rrc, Ro0], [Ryd])

        def d_tail_1(jt):
            ydf = YD[:].rearrange("p s h n -> p (s h) n")
            ysf = YSQ[:].rearrange("p s h n -> p (s h) n")
            dve(lambda: V.tensor_tensor(out=ysf, in0=ydf, in1=ydf, op=ALU.mult), [Ryd], [Rysq])
            dve(lambda: V.tensor_reduce(out=RC[:, 8:16], in_=ysf, axis=AX.X, op=ALU.add), [Rysq], [Rrct])

        def d_tail_2(jt):
            act(RC[:, 8:16], RC[:, 8:16], AF.Ln, [Rrct, Rcons], [Rrct], scale=1.0 / 128, bias=CONS[:, 0:1])
            act(RC[:, 8:16], RC[:, 8:16], AF.Exp, [Rrct], [Rrct], scale=-0.5)

        def d_tail_3(jt):
            ydf = YD[:].rearrange("p s h n -> p (s h) n")
            ysf = YSQ[:].rearrange("p s h n -> p (s h) n")
            dve(lambda: V.tensor_tensor(out=ysf, in0=ydf, in1=RC[:, 8:16].unsqueeze(2).to_broadcast([128, 8, 128]), op=ALU.mult),
                [Ryd, Rrct], [Rysq])
            dve(lambda: V.tensor_tensor(out=YB[:].rearrange("p s (h n) -> p (s h) n", h=4), in0=ysf,
                                        in1=SUBLN[:].unsqueeze(1).to_broadcast([128, 8, 128]), op=ALU.mult),
                [Rysq, Rsub], [Ryb])

        def d_tail_b(jt, halves=(0, 1)):
            for s_ in halves:
                tt = 2 * jt + s_
                ptv = bank(7).bitcast(BF16)
                for cc in range(4):
                    tr(ptv[:, cc * 128:(cc + 1) * 128], YB[:, s_, cc * 128:(cc + 1) * 128], [Ryb, Ridb], [RB[7]], sig=(cc == 3))
                dve(lambda: V.tensor_copy(out=ydfT[:, :, tt * 128:(tt + 1) * 128], in_=ptv[:, 0:512].rearrange("p (c t) -> p c t", c=4)),
                    [RB[7]], [RYdf[tt]])

        q_aug(0)
        q_proj_a(0, 0, 0)
        q_proj_a(0, 1, 1)
        q_proj_b(0, 0, 0)
        q_proj_a(0, 2, 2)
        q_proj_b(0, 1, 1)
        q_proj_a(0, 3, 3)
        q_proj_b(0, 2, 2)
        q_proj_b(0, 3, 3)
        groups = [(jt, h, c, g) for jt in range(8) for h in range(4) for c in range(2) for g in range(4)]
        for gi, (jt, h, c, g) in enumerate(groups):
            li = gi % 32
            if li == 0 and jt + 1 < 8:
                q_aug(jt + 1)
            d_front(gi, jt, h, c, g)
            if gi >= 2:
                pj = groups[gi - 2]
                d_back(gi - 2, *pj)
                if (gi - 2) % 32 == 31:
                    d_tail_1(pj[0])
            if jt >= 1:
                if li == 4:
                    d_tail_2(jt - 1)
                if li == 6:
                    d_tail_3(jt - 1)
                if li == 9:
                    d_tail_b(jt - 1, (0,))
                if li == 11:
                    d_tail_b(jt - 1, (1,))
            if jt + 1 < 8 and li in (5, 13, 21, 29):
                q_proj_a(jt + 1, (li - 5) // 8)
            if jt + 1 < 8 and li in (7, 15, 23, 31):
                q_proj_b(jt + 1, (li - 7) // 8)
        d_back(len(groups) - 2, *groups[-2])
        d_back(len(groups) - 1, *groups[-1])
        d_tail_1(7)
        d_tail_2(7)
        d_tail_3(7)
        hookE = (lambda: d_tail_b(7))
        if dbg:
            hookE()
            hookE = None
        w_release(wkq)
        if dbg and b == 0:
            dump("ydfT", ydfT, [128, 4, SEQ])

        RmT = [[R(f"mT{e}_{t}") for t in range(4)] for e in range(8)]
        Rga, Rgb, Rt1, Rt2 = [[R(f"{n}{i}") for i in range(2)] for n in ("ga", "gb", "t1", "t2")]
        fit = 0
        for ep in range(4):
            wkg, sg, Wg = w_acquire("gate")
            if ep % 2 == 0:
                wkp, sp_, Wp = w_acquire("proj2")
            pc0 = (ep % 2) * 256
            def f_gate(e2, tq, i):
                b0 = 4 * i
                tok = slice(tq * 512, (tq + 1) * 512)
                for dc in range(8):
                    mm(bank(b0), Wg[:, dc, e2 * 128:(e2 + 1) * 128], hT[:, dc, tok], dc == 0, dc == 7, [RW[sg]], [RB[b0]])
                for dc in range(8):
                    mm(bank(b0 + 1), Wg[:, dc, 256 + e2 * 128:256 + (e2 + 1) * 128], hT[:, dc, tok], dc == 0, dc == 7, [RW[sg]], [RB[b0 + 1]])

            def f_proj(e2, tq, i):
                e = 2 * ep + e2
                b0 = 4 * i
                tok = slice(tq * 512, (tq + 1) * 512)
                for cc in range(4):
                    mm(bank(b0 + 2), Wp[:, 0, cc, pc0 + e2 * 128:pc0 + (e2 + 1) * 128], ynaT[:, cc, tok], cc == 0, cc == 3,
                       [RW[sp_]] + RYna[4 * tq:4 * tq + 4], [RB[b0 + 2]])
                for cc in range(4):
                    mm(bank(b0 + 3), Wp[:, 1, cc, pc0 + e2 * 128:pc0 + (e2 + 1) * 128], ydfT[:, cc, tok], cc == 0, cc == 3,
                       [RW[sp_]] + RYdf[4 * tq:4 * tq + 4], [RB[b0 + 3]])
                act(GA[i][:], bank(b0), AF.Sigmoid, [RB[b0], Rpp], [Rga[i]], bias=PPt[:, PP_BG + e:PP_BG + e + 1])
                act(GB[i][:], bank(b0 + 1), AF.Sigmoid, [RB[b0 + 1], Rpp], [Rgb[i]], bias=PPt[:, PP_BG + 8 + e:PP_BG + 9 + e])
                dve(lambda: V.tensor_tensor(out=T1[i][:], in0=bank(b0 + 2), in1=GA[i][:], op=ALU.mult), [RB[b0 + 2], Rga[i]], [Rt1[i]])
                dve(lambda: V.tensor_tensor(out=T2[i][:], in0=bank(b0 + 3), in1=GB[i][:], op=ALU.mult), [RB[b0 + 3], Rgb[i]], [Rt2[i]])
                dve(lambda: V.tensor_tensor(out=mT[:, e, tok], in0=T1[i][:], in1=T2[i][:], op=ALU.add), [Rt1[i], Rt2[i]], [RmT[e][tq]])

            blks = [(e2, tq) for e2 in range(2) for tq in range(4)]
            bi_ = 0
            while bi_ < 8:
                if fit == 2 and hookE is not None:
                    hookE()
                    hookE = None
                if ep == 2 and bi_ == 0:
                    i0, i1 = fit % 2, (fit + 1) % 2
                    f_gate(*blks[0], i0)
                    f_gate(*blks[1], i1)
                    f_proj(*blks[0], i0)
                    f_proj(*blks[1], i1)
                    fit += 2
                    bi_ += 2
                else:
                    i0 = fit % 2
                    f_gate(*blks[bi_], i0)
                    f_proj(*blks[bi_], i0)
                    fit += 1
                    bi_ += 1
            if ep % 2 == 0:
                w_release(wkg)
            else:
                w_release(wkp)
                w_release(wkg)
        if dbg and b == 0:
            dump("mT", mT, [128, 8, SEQ])

        RX1 = [R(f"x1_{t}") for t in range(NTT)]
        Rh2T = [[R(f"hT{t}_{c}") for c in range(8)] for t in range(4)]
        Rxg, Rtg = [R("xg0"), R("xg1")], [R("tg0"), R("tg1")]
        Rxng, Rjg = R("xng"), R("junkg")
        _a0 = w_acquire("win_o")
        _a1 = w_acquire("win_o")
        so = [_a0[1], _a1[1]]
        Wo = [_a0[2], _a1[2]]
        RmT_all = [r for l in RmT for r in l]
        def phG_tile_half(tt, hf):
            xi = tt % 2
            bk = 2 * (tt % 2) + hf
            tq = tt // 4
            for dc in range(8):
                mm(bank(bk), mT[:, dc, tt * 128:(tt + 1) * 128], Wo[hf][:, dc, :], dc == 0, dc == 7,
                   [RW[so[hf]]] + ([RmT[dc][tq]]), [RB[bk]])
            cs = slice(hf * 512, (hf + 1) * 512)
            dve(lambda: V.tensor_tensor(out=TG[xi][:, cs], in0=bank(bk), in1=GBC[:, 0, cs], op=ALU.mult),
                [RB[bk], Rgbc], [Rtg[xi]])

        def phG_load(tt):
            xi = tt % 2
            S.dma("sp", XG[xi][:], x_d[b, tt * 128:(tt + 1) * 128, :], writes=[Rxg[xi]] + (RYna if tt < 2 else []))

        def phG_add(tt):
            xi = tt % 2
            dve(lambda: V.tensor_tensor(out=x1[:, tt, :], in0=TG[xi][:], in1=XG[xi][:], op=ALU.add),
                [Rtg[xi], Rxg[xi]], [RX1[tt]])

        def phG_mm(tq):
            if tq == 0:
                for p_ in range(2):
                    tA, tB = 2 * p_, 2 * p_ + 1
                    phG_load(tA)
                    phG_load(tB)
                    for hf in range(2):
                        phG_tile_half(tA, hf)
                        phG_tile_half(tB, hf)
                    phG_add(tA)
                    phG_add(tB)
                return
            for i in range(4):
                tt = 4 * tq + i
                phG_load(tt)
                for hf in range(2):
                    phG_tile_half(tt, hf)
                phG_add(tt)

        def phG_n(tq, part):
            norm_to_T(b, [x1[:, 4 * tq + i, :] for i in range(4)], RX1[4 * tq:4 * tq + 4], A2, 24, hT, Rh2T, XNG, Rxng,
                      JUNKG, Rjg, tq, [4, 5], part=part)

        phG_mm(0)
        phG_n(0, 1)
        for tq in range(1, 4):
            phG_mm(tq)
            phG_n(tq - 1, 2)
            phG_n(tq, 1)
        hookG = (lambda: phG_n(3, 2))
        if dbg:
            hookG()
            hookG = None
        w_release(_a0[0])
        w_release(_a1[0])
        if dbg and b == 0:
            dump("x1", x1, [128, NTT, D])
            dump("h2T", hT, [128, 8, SEQ])

        Racc = [[R("acc0a"), R("acc0b")], [R("acc1a"), R("acc1b")]]
        Rgg = [R("gga"), R("ggb")]
        Rtd = [R("td0"), R("td1")]
        H2 = SEQ // 2
        for (f0, f1) in THIRDS:
            RaT = [[R(f"aT{i}a"), R(f"aT{i}b")] for i in range(f1 - f0)]
            fc = f0
            while fc < f1:
                npair = min(2, f1 - fc)
                wku, su, Wu = w_acquire("wup")
                for k in range(npair):
                    f = fc + k
                    for half in range(2):
                        acc, Ra = ACC[half], Racc[half]
                        pb0 = 4 * half
                        Rp = RB[pb0:pb0 + 4]
                        col0 = half * 256 + k * 128
                        for tq in range(4):
                            for dc in range(8):
                                mm(bank(pb0 + tq), Wu[:, dc, col0:col0 + 128], hT[:, dc, tq * 512:(tq + 1) * 512], dc == 0, dc == 7,
                                   [RW[su]] + (Rh2T[tq] if dc == 0 else []), [Rp[tq]])
                            if tq == 2 and hookG is not None:
                                hookG()
                                hookG = None
                        U = PS[:, 2048 * half:2048 * (half + 1)]
                        fcol = half * NFC + f
                        w0 = PPt[:, PP_CW + fcol:PP_CW + fcol + 1]
                        w1 = PPt[:, PP_CW + 44 + fcol:PP_CW + 45 + fcol]
                        w2 = PPt[:, PP_CW + 88 + fcol:PP_CW + 89 + fcol]
                        cb = PPt[:, PP_CB + fcol:PP_CB + fcol + 1]
                        if f == f1 - 1:
                            segs = [(0, H2, [Ra[0]], [Rgg[0]], [RaT[f - f0][0]], Rp[0:3], [Ra[1]]),
                                    (H2, SEQ, [Ra[1]], [Rgg[1]], [RaT[f - f0][1]], Rp[1:4], [])]
                        else:
                            segs = [(0, SEQ, Ra, Rgg, RaT[f - f0], Rp, [])]
                        for (t0, t1, ra, rg, rat, rp, xr) in segs:
                            act(acc[:, t0:t1], U[:, t0:t1], AF.Identity, rp + [Rpp], ra, scale=w1, bias=cb)
                        for (t0, t1, ra, rg, rat, rp, xr) in segs:
                            lo = max(t0, 1)
                            dve(lambda: V.scalar_tensor_tensor(out=acc[:, lo:t1], in0=U[:, lo - 1:t1 - 1], scalar=w0, in1=acc[:, lo:t1],
                                                               op0=ALU.mult, op1=ALU.add), rp + ra + [Rpp], ra)
                            hi = min(t1, SEQ - 1)
                            dve(lambda: V.scalar_tensor_tensor(out=acc[:, t0:hi], in0=U[:, t0 + 1:hi + 1], scalar=w2, in1=acc[:, t0:hi],
                                                               op0=ALU.mult, op1=ALU.add), rp + ra + xr + [Rpp], ra)
                            if half == 0:
                                act(GG[:, t0:t1], acc[:, t0:t1], AF.Gelu, ra, rg)
                            else:
                                dve(lambda: V.tensor_tensor(out=aT[:, f - f0, t0:t1], in0=acc[:, t0:t1], in1=GG[:, t0:t1], op=ALU.mult),
                                    ra + rg, rat)
                w_release(wku)
                fc += npair
            nch = f1 - f0
            sd = []
            for q in range(0, nch, 4):
                wkd, s_, Wd = w_acquire("wdn")
                sd.append((s_, Wd, wkd))
            for tt in range(NTT):
                ti = tt % 2
                for hf in range(2):
                    bk = 2 * ti + hf
                    for q in range(nch):
                        s_, Wd, _ = sd[q // 4]
                        mm(bank(bk), aT[:, q, tt * 128:(tt + 1) * 128], Wd[:, q % 4, hf * 512:(hf + 1) * 512], q == 0, q == nch - 1,
                           [RW[s_], RaT[q][tt // 8]], [RB[bk]])
                    cs = slice(hf * 512, (hf + 1) * 512)
                    dve(lambda: V.tensor_tensor(out=TD[ti][:, cs], in0=bank(bk), in1=GBC[:, 1, cs], op=ALU.mult),
                        [RB[bk], Rgbc], [Rtd[ti]])
                dve(lambda: V.tensor_tensor(out=x1[:, tt, :], in0=x1[:, tt, :], in1=TD[ti][:], op=ALU.add),
                    [Rtd[ti]], [RX1[tt]])
                if f1 == NFC:
                    S.dma("pool", out_d[b, tt * 128:(tt + 1) * 128, :], x1[:, tt, :], reads=[RX1[tt]])
            for (_s, _w, _k) in sd:
                w_release(_k)
        out_res = RX1

    nc._dbg_names = list(dbg_d)
    S.finish(out_res)
    st.close()
    return nc


_CONST_CACHE = {}


def host_consts():
    if _CONST_CACHE:
        return _CONST_CACHE
    c = {}
    c["identf"] = np.eye(128, dtype=np.float32)
    bd = np.zeros((128, 128), np.float32)
    bd[:64, :64] = 1.0
    bd[64:, 64:] = 1.0
    c["bdones"] = bd
    sel = np.zeros((4, 4, 128), np.float32)
    for b in range(4):
        sel[b, b, :] = 1.0
    c["onesel"] = sel.reshape(4, 512)
    slopes = np.array([2.0 ** (-8.0 * (h + 1) / 4) for h in range(4)], np.float64)
    kk = np.arange(128)
    dg = np.zeros((128, 4, 128), np.float32)
    for h in range(4):
        dg[:, h, :] = -8.0 * slopes[h] * np.abs(kk[None, :] - kk[:, None])
    c["diagb"] = dg.reshape(128, 512)
    pos = np.arange(SEQ)
    hi, lo = (pos // 64) * 64, pos % 64
    kaug = np.zeros((8, 8, SEQ), np.float32)
    qaug = np.zeros((8, 8, SEQ), np.float32)
    for h in range(4):
        m8 = 8.0 * slopes[h]
        for cc in range(2):
            hc = 2 * h + cc
            for r0 in (0, 4):
                kaug[r0 + 0, hc] = 1.0
                kaug[r0 + 1, hc] = 1.0
                kaug[r0 + 2, hc] = m8 * hi
                kaug[r0 + 3, hc] = m8 * lo
            qaug[0, hc] = -m8 * hi
            qaug[1, hc] = -m8 * lo
            qaug[2, hc] = 1.0
            qaug[3, hc] = 1.0
            qaug[4, hc] = 2.0 * m8 * hi
            qaug[5, hc] = 2.0 * m8 * lo
            qaug[6, hc] = -2.0
            qaug[7, hc] = -2.0
    c["kaug"] = kaug.reshape(8, 8 * SEQ)
    qa = qaug.reshape(8, 8, 8, 256)
    c["qaug"] = np.ascontiguousarray(qa.transpose(2, 0, 1, 3)).reshape(8, 8, 8 * 256)
    _CONST_CACHE.update(c)
    return c


def na_table(rpb):
    cq = np.arange(64)
    cs = np.clip(cq - 8, 0, 48)
    ck = np.arange(64)
    colvalid = (ck[:, None] >= cs[None, :]) & (ck[:, None] < cs[None, :] + 16)
    dc = ck[:, None] - cq[None, :] + 15
    dcc = np.clip(dc, 0, 30)
    tab = np.full((8, 2, 64, 12, 2, 64), NEG, np.float32)
    for slot in range(12):
        if slot < 7:
            jp, masked = slot - 1, False
        else:
            jp, masked = slot - 7, True
        for half in range(2):
            for par in range(2):
                dr = 2 * jp + half - 4 - par
                if abs(dr) > 7:
                    continue
                if masked and not (-4 <= dr <= 3):
                    continue
                vals = rpb[:, dr + 7, :][:, dcc]
                tab[:, half, :, slot, par, :] = np.where(colvalid[None], vals, np.float32(NEG))
    return np.ascontiguousarray(tab.reshape(8, 128, 12 * 2 * 64))


_PROG = {}


def kernel(x, c, ada_w, ada_b, norm1_g, w_in, na_q_g, na_k_g, na_rpb, df_q_g, df_k_g,
           lam_q1, lam_k1, lam_q2, lam_k2, df_subln_g, w_na_proj, w_df_proj, w_gate, b_gate,
           w_out, norm2_g, w_up, conv_w, conv_b, w_down, _ncores=8, _nseq=4, _dbg=False):
    f = lambda a: np.ascontiguousarray(np.asarray(a, dtype=np.float32))
    x = f(x); c = f(c)
    hc = host_consts()
    pp = np.zeros((128, PP_COLS), np.float32)
    pp[:, PP_N1G:PP_N1G + 8] = f(norm1_g)[0].reshape(8, 128).T
    pp[:, PP_N2G:PP_N2G + 8] = f(norm2_g)[0].reshape(8, 128).T
    pp[:, PP_BG:PP_BG + 16] = f(b_gate)[0].reshape(16, 128).T
    cw = f(conv_w)[0]
    for i in range(3):
        pp[:, PP_CW + 44 * i:PP_CW + 44 * (i + 1)] = cw[i].reshape(44, 128).T
    pp[:, PP_CB:PP_CB + 44] = f(conv_b)[0].reshape(44, 128).T
    for k, g in enumerate((na_q_g, na_k_g, df_q_g, df_k_g)):
        pp[:, PP_G + k] = np.concatenate([f(g)[0], f(g)[0]])
    lamv = np.concatenate([f(lam_q1)[0], f(lam_q2)[0], f(lam_k1)[0], f(lam_k2)[0]])[None, :]
    lamv = np.ascontiguousarray(np.broadcast_to(lamv, (128, 256)))
    subln = np.ascontiguousarray(np.broadcast_to(f(df_subln_g)[0][None, :], (128, 128)))
    natab = na_table(f(na_rpb)[0])
    shared = {
        "ada_w": f(ada_w)[0], "ada_b": f(ada_b)[0][None, :], "w_in": f(w_in)[0], "w_na_proj": f(w_na_proj)[0],
        "w_df_proj": f(w_df_proj)[0], "w_gate": f(w_gate)[0], "w_out": f(w_out)[0], "w_up": f(w_up)[0],
        "w_down": f(w_down)[0], "pp": pp, "lamv": lamv, "subln": subln, "natab": natab,
    }
    shared.update(hc)
    key = (_nseq, _dbg)
    if key not in _PROG:
        _PROG[key] = build_program(_nseq, _dbg)
    nc = _PROG[key]
    in_maps = []
    for i in range(_ncores):
        xs = x[i * _nseq:(i + 1) * _nseq]
        cs = c[i * _nseq:(i + 1) * _nseq]
        cT = np.ascontiguousarray(cs.reshape(_nseq, 8, 128).transpose(2, 1, 0))
        m = dict(shared)
        m["x"] = xs
        m["cT"] = cT
        in_maps.append(m)
    res = run_bass_kernel_spmd(nc, in_maps, core_ids=list(range(_ncores)))
    outs = [r["out"] for r in res.results]
    return np.concatenate(outs, axis=0).astype(np.float32)
```
